# Optimizing a Trainium2 kernel written in Bass

```python
import math
import jax, jax.numpy as jnp
from jax import lax
import numpy as np


D_MODEL = 1024
BATCH = 8
SEQ = 2048
DEPTH = 1
DEC_BATCH = 128
DEC_SEQ = 8
PAST_LEN = 16384
PAGE_SIZE = 128

RET_HEADS = 4
RET_DK = 128
RET_DV = 128
RET_QK_W = RET_HEADS * RET_DK
RET_V_W = RET_HEADS * RET_DV
RET_CHUNK = 128
ROPE_BASE = 10000.0
SSM_W = D_MODEL // 2
SSM_GROUP = 16
SSM_GROUPS = SSM_W // SSM_GROUP
SSM_STATE = 64
DT_MIN = 1e-3
DT_MAX = 1e-1
D_FF = 4 * D_MODEL
N_BRANCH = 2
EPS = 1e-6
IN_SPLITS = (RET_QK_W, 2 * RET_QK_W, 2 * RET_QK_W + RET_V_W, 2 * RET_QK_W + 2 * RET_V_W,
             2 * RET_QK_W + 2 * RET_V_W + SSM_W, 2 * RET_QK_W + 2 * RET_V_W + SSM_W + D_MODEL)
IN_COLS = 2 * RET_QK_W + 2 * RET_V_W + SSM_W + N_BRANCH * D_MODEL

kernel_name = 'retnet_s5_gated_hybrid_step'


def rms_normalize(x):
    xf = x.astype(jnp.float32)
    return xf * lax.rsqrt(jnp.mean(xf * xf, axis=-1, keepdims=True) + EPS)


def rms_norm(x, g):
    return (rms_normalize(x) * g.astype(jnp.float32)).astype(x.dtype)


def rotary(x, pos):
    half = x.shape[-1] // 2
    inv = ROPE_BASE ** (-jnp.arange(half, dtype=jnp.float32) / half)
    ang = pos.astype(jnp.float32)[:, None] * inv[None, :]
    cos = jnp.cos(ang)[None, :, None, :]
    sin = jnp.sin(ang)[None, :, None, :]
    xf = x.astype(jnp.float32)
    x1, x2 = xf[..., :half], xf[..., half:]
    return jnp.concatenate([x1 * cos - x2 * sin, x1 * sin + x2 * cos], axis=-1)


def retention(q, k, v, r0):
    b, l, h, dk = q.shape
    dv = v.shape[-1]
    c = math.gcd(l, RET_CHUNK)
    n = l // c
    log_g = jnp.log1p(-(2.0 ** (-5.0 - jnp.arange(h, dtype=jnp.float32))))
    idx = jnp.arange(c, dtype=jnp.float32)
    rel = idx[:, None] - idx[None, :]
    decay = jnp.where(rel >= 0, jnp.exp(log_g[:, None, None] * jnp.maximum(rel, 0.0)), 0.0)
    q_in = jnp.exp(log_g[:, None] * (idx[None, :] + 1.0))
    k_out = jnp.exp(log_g[:, None] * (c - 1.0 - idx[None, :]))
    chunk_decay = jnp.exp(log_g * c)
    qc = q.reshape(b, n, c, h, dk)
    kc = k.reshape(b, n, c, h, dk)
    vc = v.reshape(b, n, c, h, dv)
    scores = jnp.einsum('bnihd,bnjhd->bnhij', qc, kc) * decay
    intra = jnp.einsum('bnhij,bnjhe->bnihe', scores, vc)
    kv = jnp.einsum('bnjhd,hj,bnjhe->bnhde', kc, k_out, vc)

    def step(r, kv_n):
        return chunk_decay[None, :, None, None] * r + kv_n, r

    r_last, r_prev = lax.scan(step, r0, jnp.moveaxis(kv, 1, 0))
    r_prev = jnp.moveaxis(r_prev, 0, 1)
    inter = jnp.einsum('bnihd,bnhde,hi->bnihe', qc, r_prev, q_in)
    return (intra + inter).reshape(b, l, h, dv), r_last


def s5(u, h0_re, h0_im, lam_re, lam_im, log_dt, b_re, b_im, c_re, c_im, d_skip):
    bsz, l, _ = u.shape
    uf = u.astype(jnp.float32).reshape(bsz, l, SSM_GROUPS, SSM_GROUP)
    lam = lax.complex(lam_re.astype(jnp.float32), lam_im.astype(jnp.float32))
    dt = jnp.exp(log_dt.astype(jnp.float32))[:, None]
    lam_bar = jnp.exp(lam * dt)
    b_mat = lax.complex(b_re.astype(jnp.float32), b_im.astype(jnp.float32))
    c_mat = lax.complex(c_re.astype(jnp.float32), c_im.astype(jnp.float32))
    b_bar = ((lam_bar - 1.0) / lam)[..., None] * b_mat
    bu = jnp.einsum('blgm,gpm->blgp', uf, b_bar)
    h0 = lax.complex(h0_re.astype(jnp.float32), h0_im.astype(jnp.float32))
    bu = bu.at[:, 0].add(lam_bar[None] * h0)
    a = jnp.broadcast_to(lam_bar, bu.shape)

    def combine(e1, e2):
        a1, b1 = e1
        a2, b2 = e2
        return a1 * a2, a2 * b1 + b2

    _, hs = lax.associative_scan(combine, (a, bu), axis=1)
    y = jnp.einsum('blgp,gmp->blgm', hs, c_mat).real + d_skip.astype(jnp.float32) * uf
    h_last = hs[:, -1]
    return y.reshape(bsz, l, SSM_W), h_last.real, h_last.imag


def hybrid_layer(x, c, r0, s0_re, s0_im, pos0, w_ada, b_ada, norm1_g, norm2_g, w_in,
                 lam_re, lam_im, log_dt, b_re, b_im, c_re, c_im, d_skip,
                 w_glu, w_br, w_bs, w_out, w_up, w_down):
    bsz, seq_len, _ = x.shape
    mod = (jnp.einsum('bd,de->be', jax.nn.silu(c), w_ada) + b_ada).astype(x.dtype)
    sh1, sc1, gt1, sh2, sc2, gt2 = jnp.split(mod[:, None, :], 6, axis=-1)
    h = rms_norm(x, norm1_g) * (1 + sc1) + sh1
    z = jnp.einsum('bld,de->ble', h, w_in)
    q, k, v, g, u, gate_r, gate_s = jnp.split(z, IN_SPLITS, axis=-1)
    pos = pos0 + jnp.arange(seq_len, dtype=jnp.int32)
    q = rotary(q.reshape(bsz, seq_len, RET_HEADS, RET_DK), pos)
    k = rotary(k.reshape(bsz, seq_len, RET_HEADS, RET_DK), pos) * (RET_DK ** -0.5)
    ret, r_new = retention(q, k, v.reshape(bsz, seq_len, RET_HEADS, RET_DV).astype(jnp.float32),
                           r0.astype(jnp.float32))
    ret = rms_normalize(ret).reshape(bsz, seq_len, RET_V_W) * jax.nn.silu(g.astype(jnp.float32))
    ssm, s_re, s_im = s5(u, s0_re, s0_im, lam_re, lam_im, log_dt, b_re, b_im, c_re, c_im, d_skip)
    ssm = jax.nn.gelu(ssm).astype(x.dtype)
    ssm = ssm * jax.nn.sigmoid(jnp.einsum('blc,ce->ble', ssm, w_glu))
    branch_r = jnp.einsum('blc,cd->bld', ret.astype(x.dtype), w_br)
    branch_s = jnp.einsum('blc,cd->bld', ssm, w_bs)
    merged = jax.nn.sigmoid(gate_r) * branch_r + jax.nn.sigmoid(gate_s) * branch_s
    x = x + gt1 * jnp.einsum('bld,de->ble', merged, w_out)
    h2 = rms_norm(x, norm2_g) * (1 + sc2) + sh2
    ff = jnp.einsum('blf,fd->bld', jnp.square(jax.nn.relu(jnp.einsum('bld,df->blf', h2, w_up))), w_down)
    x = x + gt2 * ff
    return x, r_new, s_re, s_im


def setup_inputs(seed: int = 0) -> dict:
    key = jax.random.key(seed)
    ks = jax.random.split(key, 28)
    f32 = jnp.float32
    nrm = lambda k, shape, s: jax.random.normal(k, shape, f32) * s
    n_idx = jnp.arange(SSM_STATE, dtype=f32)
    lam_re = -0.5 + nrm(ks[10], (DEPTH, SSM_GROUPS, SSM_STATE), 0.01)
    lam_im = math.pi * n_idx[None, None, :] + nrm(ks[11], (DEPTH, SSM_GROUPS, SSM_STATE), 0.01)
    log_dt = jax.random.uniform(ks[12], (DEPTH, SSM_GROUPS), f32, math.log(DT_MIN), math.log(DT_MAX))
    return {
        'x_prompt': nrm(ks[0], (BATCH, SEQ, D_MODEL), 1.0),
        'x_sample': nrm(ks[1], (DEC_BATCH, DEC_SEQ, D_MODEL), 1.0),
        'c_prompt': nrm(ks[2], (BATCH, D_MODEL), 1.0),
        'c_sample': nrm(ks[3], (DEC_BATCH, D_MODEL), 1.0),
        'state_ret': nrm(ks[4], (DEPTH, DEC_BATCH, RET_HEADS, RET_DK, RET_DV), 1.0),
        'state_ssm_re': nrm(ks[5], (DEPTH, DEC_BATCH, SSM_GROUPS, SSM_STATE), 0.5),
        'state_ssm_im': nrm(ks[6], (DEPTH, DEC_BATCH, SSM_GROUPS, SSM_STATE), 0.5),
        'w_ada': nrm(ks[7], (DEPTH, D_MODEL, 6 * D_MODEL), D_MODEL ** -0.5),
        'b_ada': nrm(ks[8], (DEPTH, 6 * D_MODEL), 0.02),
        'norm1_g': 1.0 + nrm(ks[9], (DEPTH, D_MODEL), 0.02),
        'norm2_g': 1.0 + nrm(ks[13], (DEPTH, D_MODEL), 0.02),
        'w_in': nrm(ks[14], (DEPTH, D_MODEL, IN_COLS), D_MODEL ** -0.5),
        'ssm_lambda_re': lam_re,
        'ssm_lambda_im': lam_im,
        'ssm_log_dt': log_dt,
        'ssm_b_re': nrm(ks[15], (DEPTH, SSM_GROUPS, SSM_STATE, SSM_GROUP), (2 * SSM_GROUP) ** -0.5),
        'ssm_b_im': nrm(ks[16], (DEPTH, SSM_GROUPS, SSM_STATE, SSM_GROUP), (2 * SSM_GROUP) ** -0.5),
        'ssm_c_re': nrm(ks[17], (DEPTH, SSM_GROUPS, SSM_GROUP, SSM_STATE), (2 * SSM_STATE) ** -0.5),
        'ssm_c_im': nrm(ks[18], (DEPTH, SSM_GROUPS, SSM_GROUP, SSM_STATE), (2 * SSM_STATE) ** -0.5),
        'ssm_d': nrm(ks[19], (DEPTH, SSM_GROUPS, SSM_GROUP), 1.0),
        'w_glu': nrm(ks[20], (DEPTH, SSM_W, SSM_W), SSM_W ** -0.5),
        'w_br': nrm(ks[21], (DEPTH, RET_V_W, D_MODEL), RET_V_W ** -0.5),
        'w_bs': nrm(ks[22], (DEPTH, SSM_W, D_MODEL), SSM_W ** -0.5),
        'w_out': nrm(ks[23], (DEPTH, D_MODEL, D_MODEL), D_MODEL ** -0.5),
        'w_up': nrm(ks[24], (DEPTH, D_MODEL, D_FF), D_MODEL ** -0.5),
        'w_down': nrm(ks[25], (DEPTH, D_FF, D_MODEL), D_FF ** -0.5),
        'norm_f_g': 1.0 + nrm(ks[26], (D_MODEL,), 0.02),
    }


def reference(x_prompt, x_sample, c_prompt, c_sample, state_ret, state_ssm_re, state_ssm_im,
              w_ada, b_ada, norm1_g, norm2_g, w_in, ssm_lambda_re, ssm_lambda_im, ssm_log_dt,
              ssm_b_re, ssm_b_im, ssm_c_re, ssm_c_im, ssm_d, w_glu, w_br, w_bs, w_out,
              w_up, w_down, norm_f_g):
    yp, ys = x_prompt, x_sample
    ret_p, sre_p, sim_p, ret_s, sre_s, sim_s = [], [], [], [], [], []
    zero_ret = jnp.zeros((x_prompt.shape[0], RET_HEADS, RET_DK, RET_DV), jnp.float32)
    zero_ssm = jnp.zeros((x_prompt.shape[0], SSM_GROUPS, SSM_STATE), jnp.float32)
    for i in range(DEPTH):
        w = (w_ada[i], b_ada[i], norm1_g[i], norm2_g[i], w_in[i], ssm_lambda_re[i], ssm_lambda_im[i],
             ssm_log_dt[i], ssm_b_re[i], ssm_b_im[i], ssm_c_re[i], ssm_c_im[i], ssm_d[i],
             w_glu[i], w_br[i], w_bs[i], w_out[i], w_up[i], w_down[i])
        yp, r_p, a_p, b_p = hybrid_layer(yp, c_prompt, zero_ret, zero_ssm, zero_ssm, 0, *w)
        ys, r_s, a_s, b_s = hybrid_layer(ys, c_sample, state_ret[i], state_ssm_re[i], state_ssm_im[i],
                                         PAST_LEN, *w)
        ret_p.append(r_p); sre_p.append(a_p); sim_p.append(b_p)
        ret_s.append(r_s); sre_s.append(a_s); sim_s.append(b_s)
    y_prompt = rms_norm(yp, norm_f_g)
    y_sample = rms_norm(ys, norm_f_g)
    new_ret_prompt = jnp.stack(ret_p)
    new_ssm_re_prompt = jnp.stack(sre_p)
    new_ssm_im_prompt = jnp.stack(sim_p)
    new_ret_sample = jnp.stack(ret_s)
    new_ssm_re_sample = jnp.stack(sre_s)
    new_ssm_im_sample = jnp.stack(sim_s)
    return (y_prompt, y_sample, new_ret_prompt, new_ssm_re_prompt, new_ssm_im_prompt,
            new_ret_sample, new_ssm_re_sample, new_ssm_im_sample)
```

```python
import math
from contextlib import ExitStack
import numpy as np
import concourse.bass as bass
import concourse.mybir as mybir
from concourse.bass_utils import run_bass_kernel_spmd

F32 = mybir.dt.float32
BF16 = mybir.dt.bfloat16
I32 = mybir.dt.int32
AF = mybir.ActivationFunctionType
ALU = mybir.AluOpType

ENGS = ['pe', 'act', 'dve', 'pool', 'sp']
SYNC_LAT = 0.35
SLACK = 1.0
D = 1024
NCORE = 8
EPS = 1e-6
PAST_LEN = 16384
TWO_PI = 2.0 * math.pi


class Sched:
    def __init__(self):
        self.ops = []
        self.last_w = {}
        self.readers = {}
        self.phase = 0
        self.last_dma_on_key = {}
        self.last_q = {}
        self.dma_keys = []

    def barrier(self):
        self.phase += 1
        self.last_w = {}
        self.readers = {}

    def add(self, eng, fn, reads=(), writes=(), dma=None, ndma=1, cost=None, lat=None, nbytes=None):
        idx = len(self.ops)
        deps = set()
        for k in reads:
            if k in self.last_w:
                deps.add(self.last_w[k])
        for k in writes:
            if k in self.last_w:
                deps.add(self.last_w[k])
            deps.update(self.readers.get(k, ()))
        order = set()
        if dma is not None:
            if dma not in self.last_dma_on_key:
                self.dma_keys.append(dma)
            else:
                order.add(self.last_dma_on_key[dma])
            self.last_dma_on_key[dma] = idx
        for k in writes:
            self.last_w[k] = idx
            self.readers[k] = []
        for k in reads:
            if k not in writes:
                self.readers.setdefault(k, []).append(idx)
        if cost is None:
            cost = getattr(fn, 'cost', None)
        if cost is None:
            cost = {'pe': 0.5, 'act': 0.7, 'dve': 0.7, 'pool': 1.3, 'sp': 0.05}[eng]
        if dma is not None:
            nb_ = nbytes if nbytes else 65536 * ndma
            cost = max(0.05, nb_ / (230e3 if eng == 'sp' else 170e3))
            if lat is None:
                lat = 2.5
        self.ops.append(dict(eng=eng, fn=fn, deps=deps, ord=order, dma=dma, ndma=ndma, cost=cost,
                             lat=(lat if lat is not None else 0.0), phase=self.phase))
        return idx

    def schedule(self):
        import os
        if os.environ.get('KNOSCHED'):
            return list(range(len(self.ops)))
        ops = self.ops
        order = []
        fin = {}
        tfree = {e: 0.0 for e in ENGS}
        nph = self.phase + 1
        for ph in range(nph):
            idxs = [i for i, o in enumerate(ops) if o['phase'] == ph]
            if not idxs:
                continue
            t0 = max([tfree[e] for e in ENGS] + [fin[i] for i in order[-200:]] + [0.0])
            for e in ENGS:
                tfree[e] = t0
            inph = set(idxs)
            npred = {}
            succ = {}
            for i in idxs:
                ps = [d for d in (ops[i]['deps'] | ops[i]['ord']) if d in inph]
                npred[i] = len(ps)
                for d in ps:
                    succ.setdefault(d, []).append(i)
            tail = {}
            for i in reversed(idxs):
                t_ = 0.0
                for j in succ.get(i, ()):
                    if tail[j] > t_:
                        t_ = tail[j]
                tail[i] = t_ + ops[i]['cost'] + ops[i]['lat'] + SYNC_LAT
            ready = [i for i in idxs if npred[i] == 0]
            while ready:
                sts = {}
                for i in ready:
                    o = ops[i]
                    st = tfree[o['eng']]
                    for d in o['deps']:
                        if d in fin and fin[d] + SYNC_LAT > st:
                            st = fin[d] + SYNC_LAT
                    sts[i] = st
                mn = min(sts.values())
                best = None
                for i in ready:
                    if sts[i] <= mn + SLACK:
                        if best is None or tail[i] > tail[best] + 1e-9 or (abs(tail[i] - tail[best]) <= 1e-9 and i < best):
                            best = i
                bt = sts[best]
                o = ops[best]
                ready.remove(best)
                order.append(best)
                tfree[o['eng']] = bt + o['cost']
                fin[best] = bt + o['cost'] + o['lat']
                for j in succ.get(best, ()):
                    npred[j] -= 1
                    if npred[j] == 0:
                        ready.append(j)
        assert len(order) == len(ops)
        self.est_total = max(fin.values()) if fin else 0.0
        return order

    def emit(self, nc):
        order = self.schedule()
        ops = self.ops
        cnt = {e: 0 for e in ENGS}
        dcnt = {}
        tok = {}
        for i in order:
            o = ops[i]
            if o['dma'] is None:
                cnt[o['eng']] += 1
                tok[i] = ('e', o['eng'], cnt[o['eng']])
            else:
                dcnt[o['dma']] = dcnt.get(o['dma'], 0) + 16 * o['ndma']
                tok[i] = ('d', o['dma'], dcnt[o['dma']])
        nph = self.phase + 1
        ph_end = []
        c2 = {e: 0 for e in ENGS}
        d2 = {}
        byphase = {ph: [] for ph in range(nph)}
        for i in order:
            byphase[ops[i]['phase']].append(i)
        for ph in range(nph):
            for i in byphase[ph]:
                t = tok[i]
                if t[0] == 'e':
                    c2[t[1]] = t[2]
                else:
                    d2[t[1]] = t[2]
            ph_end.append(([('e', e, c2[e]) for e in ENGS if c2[e] > 0] + [('d', k, v) for k, v in d2.items()]))
        with ExitStack() as es:
            esem = {e: es.enter_context(nc.semaphore('s_' + e)) for e in ENGS}
            dsem = {k: es.enter_context(nc.semaphore('d_%d' % i)) for i, k in enumerate(self.dma_keys)}
            block = es.enter_context(nc.Block())
            handles = {'pe': block.tensor, 'act': block.scalar, 'dve': block.vector,
                       'pool': block.gpsimd, 'sp': block.sync}

            def make(ename):
                def body(eng):
                    known = {}
                    cur_phase = 0

                    def wait_for(toks):
                        need = {}
                        for d in toks:
                            key = (d[0], d[1])
                            if d[2] > need.get(key, 0):
                                need[key] = d[2]
                        for key, val in need.items():
                            if known.get(key, 0) >= val:
                                continue
                            known[key] = val
                            if key[0] == 'e':
                                eng.wait_ge(esem[key[1]], val)
                            else:
                                eng.wait_ge(dsem[key[1]], val)

                    for i in order:
                        o = ops[i]
                        if o['eng'] != ename:
                            continue
                        if o['phase'] != cur_phase:
                            cur_phase = o['phase']
                            wait_for(ph_end[cur_phase - 1])
                        wait_for([tok[d] for d in o['deps']])
                        if o['dma'] is None:
                            ins = o['fn'](eng)
                            ins.then_inc(esem[ename], 1)
                        else:
                            o['fn'](eng, dsem[o['dma']])
                    if ename == 'sp':
                        wait_for(ph_end[-1])
                return body

            for ename in ENGS:
                handles[ename](make(ename))


class Arena:
    def __init__(self, nc, es, name, nbytes):
        self.t = es.enter_context(nc.sbuf_tensor(name, [128, nbytes // 4], F32))
        self.cap = nbytes
        self.off = 0
        self.top = nbytes

    def reset(self, off=0):
        self.off = off
        self.top = self.cap

    def alloc_tail(self, free_shape, dt):
        n = 1
        for s_ in free_shape:
            n *= s_
        esz = 4 if dt in (F32, I32) else 2
        nb = (n * esz + 31) // 32 * 32
        self.top -= nb
        save = self.off
        self.off = self.top
        ap = self.alloc(free_shape, dt)
        self.off = save
        return ap

    def alloc(self, free_shape, dt):
        n = 1
        for s in free_shape:
            n *= s
        esz = 4 if dt in (F32, I32) else 2
        nb = (n * esz + 31) // 32 * 32
        assert self.off + nb <= getattr(self, 'top', self.cap) or self.off >= getattr(self, 'top', self.cap), ("arena overflow", self.off, nb, self.cap)
        ap = self.t[:, self.off // 4:(self.off + nb) // 4]
        self.off += nb
        if dt != F32:
            ap = ap.bitcast(dt)
        ap = ap[:, 0:n]
        if len(free_shape) == 2:
            ap = ap.rearrange("p (a b) -> p a b", a=free_shape[0])
        elif len(free_shape) == 3:
            ap = ap.rearrange("p (a b c) -> p a b c", a=free_shape[0], b=free_shape[1])
        elif len(free_shape) == 4:
            ap = ap.rearrange("p (a b c d) -> p a b c d", a=free_shape[0], b=free_shape[1], c=free_shape[2])
        return ap


def perm_r2t():
    r = np.arange(128)
    return 4 * (r % 32) + r // 32


def host_consts(ntp):
    p2t = perm_r2t()
    gam = np.exp(np.log1p(-(2.0 ** (-5.0 - np.arange(4)))))
    c = {}
    c['ident'] = np.eye(128, dtype=np.float32)
    half = 64
    inv = (np.float32(10000.0) ** (-(np.arange(half, dtype=np.float32) / np.float32(half)))).astype(np.float32)
    rot = np.zeros((ntp + 1, 128, 512), np.float32)
    for T in range(ntp + 1):
        if T < ntp:
            pos = 128 * T + p2t
        else:
            pos = PAST_LEN + (p2t % 8)
        ang = (pos.astype(np.float32)[:, None] * inv[None, :]).astype(np.float32)
        co = np.cos(ang.astype(np.float64)); si = np.sin(ang.astype(np.float64))
        tq = np.concatenate([co, co, -si, si], axis=1)
        rot[T, :, 0:256] = tq
        rot[T, :, 256:512] = tq * (128.0 ** -0.5)
    c['rot'] = rot
    maskT = np.zeros((2, 128, 4, 128), np.float32)
    qin = np.zeros((2, 128, 4, 128), np.float32)
    kout = np.zeros((2, 128, 4), np.float32)
    t = p2t
    for h in range(4):
        g = gam[h]
        dlt = t[None, :] - t[:, None]
        maskT[0, :, h, :] = np.where(dlt >= 0, g ** np.maximum(dlt, 0), 0.0)
        qin[0, :, h, :] = (g ** (t + 1.0))[None, :]
        kout[0, :, h] = g ** (127.0 - t)
        b = t // 8; pos = t % 8
        dl2 = pos[None, :] - pos[:, None]
        same = (b[None, :] == b[:, None]) & (dl2 >= 0)
        maskT[1, :, h, :] = np.where(same, g ** np.maximum(dl2, 0), 0.0)
        qin[1, :, h, :] = (g ** (pos + 1.0))[None, :]
        kout[1, :, h] = g ** (7.0 - pos)
    c['maskT'] = maskT
    c['qin'] = qin
    c['kout'] = kout
    seqf = np.zeros((128, 16, 128), np.float32)
    seqp = np.zeros((128, 16), np.float32)
    for b in range(16):
        m = (t // 8 == b).astype(np.float32)
        seqf[:, b, :] = m[None, :]
        seqp[:, b] = m
    c['seqf'] = seqf
    c['seqp'] = seqp
    jj = np.arange(128) // 32
    c['tmask'] = (jj[:, None] <= jj[None, :]).astype(np.float32)
    cd = np.stack([gam ** 128.0, gam ** 8.0])
    return c, cd


CONST_SHAPES = None


def build_program(ntp, cd):
    NT = ntp + 1
    nc = bass.Bass("TRN2", target_bir_lowering=False)
    S = Sched()

    def din(name, shape):
        return nc.dram_tensor(name, list(shape), F32, kind="ExternalInput").ap()

    def dout(name, shape):
        return nc.dram_tensor(name, list(shape), F32, kind="ExternalOutput").ap()

    xin = din("xin", [NT, 128, D])
    cexp = din("cexp", [2, 128, D])
    sret = din("sret", [16, 4, 128, 128])
    ssre = din("ssre", [16, 2048])
    ssim = din("ssim", [16, 2048])
    w_ada = din("w_ada", [D, 6 * D])
    b_ada = din("b_ada", [1, 6 * D])
    n1g = din("norm1_g", [1, D])
    n2g = din("norm2_g", [1, D])
    nfg = din("norm_f_g", [1, D])
    w_in = din("w_in", [D, 4608])
    lam_re = din("lam_re", [32, 64])
    lam_im = din("lam_im", [32, 64])
    log_dt = din("log_dt", [1, 32])
    b_re = din("b_re", [32, 64, 16])
    b_im = din("b_im", [32, 64, 16])
    c_re = din("c_re", [512, 64])
    c_im = din("c_im", [512, 64])
    ssm_d = din("ssm_d", [1, 512])
    w_glu = din("w_glu", [512, 512])
    w_br = din("w_br", [512, D])
    w_bs = din("w_bs", [512, D])
    w_out = din("w_out", [D, D])
    w_up = din("w_up", [D, 4 * D])
    w_down = din("w_down", [4 * D, D])
    k_ident = din("k_ident", [128, 128])
    k_rot = din("k_rot", [NT, 128, 512])
    k_maskT = din("k_maskT", [2, 128, 512])
    k_qin = din("k_qin", [2, 128, 512])
    k_kout = din("k_kout", [2, 128, 4])
    k_seqf = din("k_seqf", [128, 2048])
    k_seqp = din("k_seqp", [128, 16])
    k_tmask = din("k_tmask", [128, 128])

    yout = dout("yout", [NT, 128, D])
    o_retp = dout("o_retp", [4, 128, 128])
    o_srep = dout("o_srep", [32, 64])
    o_simp = dout("o_simp", [32, 64])
    o_rets = dout("o_rets", [16, 4, 128, 128])
    o_sres = dout("o_sres", [16, 2048])
    o_sims = dout("o_sims", [16, 2048])
    x1d = nc.dram_tensor("x1d", [NT, 128, D], F32, kind="Internal").ap()
    modd = nc.dram_tensor("modd", [2, 6, 128, D], F32, kind="Internal").ap()

    es = ExitStack()
    with es:
        PS = es.enter_context(nc.psum_tensor("PS", [128, 4096], F32))
        PERS = Arena(nc, es, "pers", 2 * 1024)
        AR = Arena(nc, es, "arena", 204 * 1024)

        def bank(b):
            return PS[:, 512 * b:512 * (b + 1)]

        def bankbf(b):
            return PS[:, 512 * b:512 * (b + 1)].bitcast(BF16)

        psrr = [0]
        pslo = [0]
        pshi = [8]

        def nb():
            n = pshi[0] - pslo[0]
            b = pslo[0] + psrr[0] % n
            psrr[0] += 1
            return b

        def nb2():
            if psrr[0] % 2:
                psrr[0] += 1
            b = psrr[0] % 8
            psrr[0] += 2
            return b

        def pk(b):
            return ('ps', b)

        ukey = [0]

        def uk():
            ukey[0] += 1
            return 'u%d' % ukey[0]

        def fsz(ap):
            n = 1
            for d_ in ap.shape[1:]:
                n *= d_
            return n

        def dma_load(out, in_, key, dkey=None, eng='sp', reads=(), **kw):
            S.add(eng, lambda e, s: e.dma_start(out=out, in_=in_, **kw).then_inc(s, 16),
                  reads=list(reads), writes=[key], dma=dkey or uk(), nbytes=out.shape[0] * fsz(out) * 4)

        def dma_store(out, in_, rkeys, dkey=None, eng='sp'):
            S.add(eng, lambda e, s: e.dma_start(out=out, in_=in_).then_inc(s, 16),
                  reads=list(rkeys), writes=[], dma=dkey or uk(), nbytes=in_.shape[0] * fsz(in_) * 4)

        def op(eng, fn, r=(), w=()):
            c = getattr(fn, 'cost', None)
            if isinstance(c, tuple):
                n_ = c[1]
                c = {'dve': 0.17 + n_ / 960.0, 'act': 0.28 + n_ / 1200.0, 'pool': 0.35 + n_ / 500.0}.get(eng, 0.5)
            S.add(eng, fn, reads=list(r), writes=list(w), cost=c)

        def pe_group(fns, r, w):
            fns = list(fns)

            def run(e):
                ins = None
                for f in fns:
                    ins = f(e)
                return ins
            S.add('pe', run, reads=list(r), writes=list(w), cost=sum(getattr(f, 'cost', 0.07) for f in fns))

        def withcost(f, c):
            f.cost = c
            return f

        def MM(out, lhsT, rhs, start, stop):
            return withcost(lambda e: e.matmul(out, lhsT=lhsT, rhs=rhs, start=start, stop=stop),
                            max(0.06, 0.00042 * fsz(rhs) + 0.005) * (4.0 if rhs.dtype == F32 else 1.0))

        def TR(out, in_, idn):
            return withcost(lambda e: e.transpose(out=out, in_=in_, identity=idn), 0.065)

        def TT(out, in0, in1, o):
            return withcost(lambda e: e.tensor_tensor(out=out, in0=in0, in1=in1, op=o), ('tt', fsz(out)))

        def TS(out, in0, s1, s2, o0, o1=None):
            if o1 is None:
                return withcost(lambda e: e.tensor_scalar(out=out, in0=in0, scalar1=s1, scalar2=None, op0=o0), ('tt', fsz(out)))
            return withcost(lambda e: e.tensor_scalar(out=out, in0=in0, scalar1=s1, scalar2=s2, op0=o0, op1=o1), ('tt', fsz(out)))

        def STT(out, in0, sc, in1, o0, o1):
            return withcost(lambda e: e.scalar_tensor_tensor(out=out, in0=in0, scalar=sc, in1=in1, op0=o0, op1=o1), ('tt', fsz(out)))

        def ACT(out, in_, func, scale=None, accum_out=None):
            kw = {}
            if scale is not None:
                kw['scale'] = scale
            if accum_out is not None:
                kw['accum_out'] = accum_out
            return withcost(lambda e: e.activation(out=out, in_=in_, func=func, **kw),
                            ('tt', fsz(out) + (110 if accum_out is not None else 0)))

        def CP(out, in_):
            return withcost(lambda e: e.tensor_copy(out=out, in_=in_), ('tt', fsz(out)))

        def ACP(out, in_):
            return withcost(lambda e: e.copy(out=out, in_=in_), ('tt', fsz(out)))

        def MS(out, v):
            return withcost(lambda e: e.memset(out, v), ('tt', fsz(out)))

        def cast_load_rows(dst3, src2, nkt, key, c0=0, c1=None, dkey=None):
            dk = dkey or uk()

            def run(e, s):
                for kt in range(nkt):
                    e.dma_start(out=dst3[:, kt, :], in_=src2[kt * 128:(kt + 1) * 128, c0:c1],
                                max_dma_last_dim=4096).then_inc(s, 16)
            S.add('pool', run, reads=[], writes=[key], dma=dk, ndma=nkt, nbytes=nkt * 128 * fsz(dst3[:, 0, :]) * 4)

        MUL, ADD, SUB = ALU.mult, ALU.add, ALU.subtract

        ident = PERS.alloc([128], F32)
        identb = PERS.alloc([128], BF16)
        nhalf = PERS.alloc([4], F32)
        stat = PERS.alloc([64], F32)
        dma_load(ident, k_ident, 'ident')
        op('dve', CP(identb, ident), ['ident'], ['identb'])
        op('dve', MS(nhalf, -0.5), [], ['nhalf'])

        def type_of(T_):
            return 0 if T_ < ntp else 1

        def load_mod(tiles, ty, idxs):
            for t_, ix in zip(tiles, idxs):
                dma_load(t_, modd[ty, ix], ('modA', ix))

        def frontend(xs, xkey, A_, sh_, Akeys, shkeys, hb, hT, junk32, jk=('junk32',), hbk='hb', hTk='hT', part=0):
            jk = list(jk)
            op('act', ACT(hb, xs, AF.Square, accum_out=stat[:, 0:1]), [xkey], [hbk, 'stat0'])
            op('dve', TS(stat[:, 1:2], stat[:, 0:1], 1.0 / D, EPS, MUL, ADD), ['stat0'], ['stat1'])
            op('pool', TT(stat[:, 2:3], stat[:, 1:2], nhalf[:, 0:1], ALU.pow), ['stat1', 'nhalf'], ['rstd'])
            if junk32 is None:
                junk32 = hb
                jk = [hbk]
            op('dve', STT(junk32, xs, stat[:, 2:3], A_, MUL, MUL), [xkey, 'rstd'] + list(Akeys), jk)
            op('dve', TT(hb, junk32, sh_, ADD), jk + list(shkeys), [hbk])
            if part == 1:
                return
            frontend2(hb, hT, hbk, hTk)

        def frontend2(hb, hT, hbk='hb', hTk='hT'):
            b = nb()
            pe_group([TR(bankbf(b)[:, k * 128:(k + 1) * 128], hb[:, k * 128:(k + 1) * 128], identb) for k in range(8)],
                     [hbk, 'identb'], [pk(b)])
            op('act', ACP(hT.rearrange("p a b -> p (a b)"), bankbf(b)), [pk(b)], [hTk])

        def phase0():
            AR.reset()
            cx = AR.alloc([2, D], F32)
            csg = AR.alloc([2, D], F32)
            cb = AR.alloc([2, D], BF16)
            cT = AR.alloc([2, 8, 128], BF16)
            gbc = AR.alloc([2, D], F32)
            wada = [AR.alloc([8, 512], BF16) for _ in range(2)]
            bbc = [AR.alloc([512], F32) for _ in range(2)]
            mo = [AR.alloc([512], F32) for _ in range(4)]
            mtmp = AR.alloc([512], F32)
            for ty in range(2):
                dma_load(cx[:, ty, :], cexp[ty], ('cx', ty))
            dma_load(gbc[:, 0, :], n1g.partition_broadcast(128), ('gbc', 0))
            dma_load(gbc[:, 1, :], n2g.partition_broadcast(128), ('gbc', 1))
            for ty in range(2):
                op('act', ACT(csg[:, ty, :], cx[:, ty, :], AF.Sigmoid), [('cx', ty)], [('csg', ty)])
                op('dve', TT(cb[:, ty, :], cx[:, ty, :], csg[:, ty, :], MUL), [('cx', ty), ('csg', ty)], [('cb', ty)])
                b = nb()
                pe_group([TR(bankbf(b)[:, k * 128:(k + 1) * 128], cb[:, ty, k * 128:(k + 1) * 128], identb) for k in range(8)],
                         [('cb', ty), 'identb'], [pk(b)])
                op('act', ACP(cT[:, ty, :, :].rearrange("p a b -> p (a b)"), bankbf(b)), [pk(b)], [('cT', ty)])
            for blk in range(12):
                sl = blk % 2
                which = blk // 2
                cast_load_rows(wada[sl], w_ada, 8, ('wada', sl), blk * 512, (blk + 1) * 512, dkey='wada%d' % sl)
                dma_load(bbc[sl], b_ada[:, blk * 512:(blk + 1) * 512].partition_broadcast(128), ('bbc', sl), 'bbc%d' % sl)
                for ty in range(2):
                    b = nb()
                    pe_group([MM(bank(b), cT[:, ty, k, :], wada[sl][:, k, :], k == 0, k == 7) for k in range(8)],
                             [('cT', ty), ('wada', sl)], [pk(b)])
                    half = blk % 2
                    cs = slice(half * 512, (half + 1) * 512)
                    mi_ = (blk * 2 + ty) % 4
                    dest = mo[mi_]
                    dkey = ('mo', mi_)
                    if which in (1, 4):
                        gi = 0 if which == 1 else 1
                        op('dve', STT(mtmp, bank(b), 1.0, bbc[sl], ADD, ADD), [pk(b), ('bbc', sl)], ['mtmp'])
                        op('dve', TT(dest, mtmp, gbc[:, gi, cs], MUL), ['mtmp', ('gbc', gi)], [dkey])
                    else:
                        op('dve', TT(dest, bank(b), bbc[sl], ADD), [pk(b), ('bbc', sl)], [dkey])
                    dma_store(modd[ty, which][:, cs], dest, [dkey], 'mo%d' % mi_)
            assert AR.off <= PH0_BYTES, AR.off

        crT_box = {}
        PH0_BYTES = 64 * 1024

        def passR():
            AR.reset()
            crT = AR.alloc([8, NT * 128], BF16)
            crT_box['crT'] = crT
            crT_box['keep'] = AR.off
            winR = AR.alloc([8, 3072], BF16)
            modR = [AR.alloc([D], F32) for _ in range(2)]
            wbr = AR.alloc([4, D], BF16)
            xs2 = [AR.alloc([D], F32) for _ in range(2)]
            rot2 = [AR.alloc([512], F32) for _ in range(2)]
            junk32 = AR.alloc([D], F32)
            hb = AR.alloc([D], BF16)
            hT = AR.alloc([8, 128], BF16)
            maskT = AR.alloc([2, 4, 128], F32)
            qinb = AR.alloc([2, 4, 128], BF16)
            koutp = AR.alloc([2, 4], F32)
            seqf = AR.alloc([16, 128], BF16)
            seqp = AR.alloc([16], F32)
            rt = [AR.alloc([4, 128], F32) for _ in range(2)]
            qr = AR.alloc([4, 128], BF16)
            kr = AR.alloc([4, 128], BF16)
            kd = AR.alloc([4, 128], BF16)
            vb = AR.alloc([512], BF16)
            sg = AR.alloc([512], F32)
            gs = AR.alloc([4, 128], F32)
            qkT = AR.alloc([8, 128], BF16)
            qsT = AR.alloc([4, 128], BF16)
            Pm = AR.alloc([4, 128], BF16)
            retf = AR.alloc([4, 128], BF16)
            retT = AR.alloc([4, 128], BF16)
            Rf = AR.alloc([4, 128], F32)
            Rb = AR.alloc([4, 128], BF16)
            sgr = AR.alloc([8, 128], BF16)
            R0f = [AR.alloc([4, 128], F32) for _ in range(2)]
            R0b = [AR.alloc([4, 128], BF16) for _ in range(2)]
            qsm = [AR.alloc([4, 128], BF16) for _ in range(2)]
            kdm = [AR.alloc([4, 128], BF16) for _ in range(2)]
            Rn = [AR.alloc([4, 128], F32) for _ in range(2)]

            cast_load_rows(winR[:, :, 0:2048], w_in, 8, 'winR0', 0, 2048)
            cast_load_rows(winR[:, :, 2048:3072], w_in, 8, 'winR1', 2560, 3584)
            cast_load_rows(wbr, w_br, 4, 'wbr')
            mT2 = maskT.rearrange("p t h i -> p t (h i)")
            dma_load(mT2[:, 0, :], k_maskT[0], ('maskT', 0))
            dma_load(mT2[:, 1, :], k_maskT[1], ('maskT', 1))
            for ty in range(2):
                dma_load(qinb[:, ty].rearrange("p h i -> p (h i)"), k_qin[ty], ('qinb', ty), eng='pool')
                dma_load(koutp[:, ty, :], k_kout[ty], ('koutp', ty))
            dma_load(seqf.rearrange("p b i -> p (b i)"), k_seqf, 'seqf', eng='pool', max_dma_last_dim=4096)
            dma_load(seqp, k_seqp, 'seqp')
            op('dve', MS(Rf, 0.0), [], ['Rf'])
            op('dve', MS(Rb, 0.0), [], ['Rb'])

            def load_x(T_):
                sl = T_ % 2
                dma_load(xs2[sl], xin[T_], ('xs', sl), 'xR%d' % sl)
                dma_load(rot2[sl], k_rot[T_], ('rot', sl), 'rot%d' % sl)

            winS_pf = AR.alloc_tail([8, 1536], BF16)
            wglu_pf = AR.alloc_tail([4, 512], BF16)
            cast_load_rows(winS_pf[:, :, 0:512], w_in, 8, 'winS0', 2048, 2560)
            cast_load_rows(winS_pf[:, :, 512:1536], w_in, 8, 'winS1', 3584, 4608)
            cast_load_rows(wglu_pf, w_glu, 4, 'wglu')
            load_x(0)
            load_mod(modR, 0, [0, 1])
            pbanks = {}
            sgr2 = [sgr, AR.alloc([8, 128], BF16)]
            hbj = AR.alloc([512], BF16)

            def stA(T_):
                ty = type_of(T_)
                sl = T_ % 2
                if T_ + 1 < NT:
                    load_x(T_ + 1)
                if T_ == ntp:
                    load_mod(modR, 1, [0, 1])
                frontend(xs2[sl], ('xs', sl), modR[1], modR[0], [('modA', 1)], [('modA', 0)], hb, hT, junk32)
                bq, bk, bv, bg = 0, 1, 2, 3
                for bi, bb in enumerate((bq, bk, bv, bg)):
                    pe_group([MM(bank(bb), hT[:, k, :], winR[:, k, bi * 512:(bi + 1) * 512], k == 0, k == 7) for k in range(8)],
                             ['hT', 'winR0'], [pk(bb)])
                pbanks[T_] = (bq, bk, bv, bg)
                for hh in range(2):
                    bb = nb()
                    fns = []
                    for mi in range(4):
                        m = hh * 4 + mi
                        for k in range(8):
                            fns.append(MM(bank(bb)[:, mi * 128:(mi + 1) * 128], winR[:, k, 2048 + m * 128:2048 + (m + 1) * 128], hT[:, k, :], k == 0, k == 7))
                    pe_group(fns, ['hT', 'winR1'], [pk(bb)])
                    op('act', ACT(sgr2[sl][:, hh * 4:(hh + 1) * 4, :].rearrange("p a b -> p (a b)"), bank(bb), AF.Sigmoid), [pk(bb)], [('sgr', sl, hh)])

            def stB1(T_):
                ty = type_of(T_)
                sl = T_ % 2
                rot = rot2[sl]
                rk = ('rot', sl)
                bq, bk, bv, bg = pbanks[T_]
                for (bb, off, dst, dkey) in ((bq, 0, qr, 'qr'), (bk, 256, kr, 'kr')):
                    src3 = bank(bb).rearrange("p (h d) -> p h d", h=4)
                    c2 = rot[:, off:off + 128].unsqueeze(1).to_broadcast([128, 4, 128])
                    sn = rot[:, off + 128:off + 192].unsqueeze(1).to_broadcast([128, 4, 64])
                    sp_ = rot[:, off + 192:off + 256].unsqueeze(1).to_broadcast([128, 4, 64])
                    op('dve', TT(rt[0], src3, c2, MUL), [pk(bb), rk], ['rt0'])
                    op('dve', TT(rt[1][:, :, 0:64], src3[:, :, 64:128], sn, MUL), [pk(bb), rk], ['rt1'])
                    op('dve', TT(rt[1][:, :, 64:128], src3[:, :, 0:64], sp_, MUL), [pk(bb), rk], ['rt1'])
                    op('pool', TT(dst, rt[0], rt[1], ADD), ['rt0', 'rt1'], [dkey])
                op('pool', TT(kd, kr, koutp[:, ty, :].unsqueeze(2).to_broadcast([128, 4, 128]), MUL), ['kr', ('koutp', ty)], ['kd'])
                op('act', ACP(vb, bank(bv)), [pk(bv)], ['vb'])
                op('act', ACT(sg, bank(bg), AF.Sigmoid), [pk(bg)], ['sg'])
                op('dve', TT(gs.rearrange("p h d -> p (h d)"), bank(bg), sg, MUL), [pk(bg), 'sg'], ['gs'])

            def stB2(T_):
                ty = type_of(T_)
                sl = T_ % 2
                bt = nb()
                pe_group([TR(bankbf(bt)[:, h * 128:(h + 1) * 128], qr[:, h, :], identb) for h in range(4)] +
                         [TR(bankbf(bt)[:, (4 + h) * 128:(5 + h) * 128], kr[:, h, :], identb) for h in range(4)],
                         ['qr', 'kr', 'identb'], [pk(bt)])
                op('act', ACP(qkT.rearrange("p a b -> p (a b)"), bankbf(bt)), [pk(bt)], ['qkT'])
                op('pool', TT(qsT, qkT[:, 0:4, :], qinb[:, ty], MUL), ['qkT', ('qinb', ty)], ['qsT'])
                bs_ = nb()
                pe_group([MM(bank(bs_)[:, h * 128:(h + 1) * 128], qkT[:, 4 + h, :], qkT[:, h, :], True, True) for h in range(4)],
                         ['qkT'], [pk(bs_)])
                op('dve', TT(Pm, bank(bs_).rearrange("p (h i) -> p h i", h=4), maskT[:, ty], MUL), [pk(bs_), ('maskT', ty)], ['Pm'])
                if ty == 0:
                    bo = nb()
                    fns = []
                    for h in range(4):
                        fns.append(MM(bank(bo)[:, h * 128:(h + 1) * 128], Pm[:, h, :], vb[:, h * 128:(h + 1) * 128], True, False))
                        fns.append(MM(bank(bo)[:, h * 128:(h + 1) * 128], qsT[:, h, :], Rb[:, h, :], False, True))
                    pe_group(fns, ['Pm', 'vb', 'qsT', 'Rb'], [pk(bo)])
                    osl = [bank(bo)[:, h * 128:(h + 1) * 128] for h in range(4)]
                    okeys = [pk(bo)]
                else:
                    bos = [0, 1, 2, 3]
                    pe_group([MM(bank(bos[h])[:, 0:128], Pm[:, h, :], vb[:, h * 128:(h + 1) * 128], True, False) for h in range(4)],
                             ['Pm', 'vb'], [pk(x) for x in bos])
                    for b_ in range(16):
                        s2 = b_ % 2
                        dma_load(R0f[s2], sret[b_].rearrange("h d e -> d h e"), ('R0f', s2), 'R0fa%d' % s2)
                        op('act', ACP(R0b[s2], R0f[s2]), [('R0f', s2)], [('R0b', s2)])
                        op('dve' if b_ % 2 else 'pool', TT(qsm[s2], qsT, seqf[:, b_, :].unsqueeze(1).to_broadcast([128, 4, 128]), MUL), ['qsT', 'seqf'], [('qsm', s2)])
                        pe_group([MM(bank(bos[h])[:, 0:128], qsm[s2][:, h, :], R0b[s2][:, h, :], False, b_ == 15) for h in range(4)],
                                 [('qsm', s2), ('R0b', s2)], [pk(x) for x in bos])
                        op('act', ACT(kdm[s2], kd, AF.Copy, scale=seqp[:, b_:b_ + 1]), ['kd', 'seqp'], [('kdm', s2)])
                        bkv = nb()
                        pe_group([MM(bank(bkv)[:, h * 128:(h + 1) * 128], kdm[s2][:, h, :], vb[:, h * 128:(h + 1) * 128], True, True) for h in range(4)],
                                 [('kdm', s2), 'vb'], [pk(bkv)])
                        for h in range(4):
                            op('dve', STT(Rn[s2][:, h, :], R0f[s2][:, h, :], float(cd[1][h]), bank(bkv)[:, h * 128:(h + 1) * 128], MUL, ADD),
                               [pk(bkv), ('R0f', s2)], [('Rn', s2)])
                        dma_store(o_rets[b_].rearrange("h d e -> d h e"), Rn[s2], [('Rn', s2)], 'o_rets%d' % s2)
                    osl = [bank(bos[h])[:, 0:128] for h in range(4)]
                    okeys = [pk(x) for x in bos]
                for h in range(4):
                    op('act', ACT(hbj[:, h * 128:(h + 1) * 128], osl[h], AF.Square, accum_out=stat[:, 8 + h:9 + h]), okeys, ['hbj', ('ss4', h)])
                op('dve', TS(stat[:, 12:16], stat[:, 8:12], 1.0 / 128, EPS, MUL, ADD), [('ss4', h) for h in range(4)], ['ms4'])
                op('pool', TT(stat[:, 16:20], stat[:, 12:16], nhalf, ALU.pow), ['ms4', 'nhalf'], ['rstd4'])
                for h in range(4):
                    op('dve', STT(retf[:, h, :], osl[h], stat[:, 16 + h:17 + h], gs[:, h, :], MUL, MUL), okeys + ['rstd4', 'gs'], ['retf'])
                brt = nb()
                pe_group([TR(bankbf(brt)[:, h * 128:(h + 1) * 128], retf[:, h, :], identb) for h in range(4)], ['retf', 'identb'], [pk(brt)])
                op('act', ACP(retT.rearrange("p a b -> p (a b)"), bankbf(brt)[:, 0:512]), [pk(brt)], ['retT'])
                if ty == 0:
                    bkv = nb()
                    pe_group([MM(bank(bkv)[:, h * 128:(h + 1) * 128], kd[:, h, :], vb[:, h * 128:(h + 1) * 128], True, True) for h in range(4)],
                             ['kd', 'vb'], [pk(bkv)])
                    for h in range(4):
                        op('dve', STT(Rf[:, h, :], Rf[:, h, :], float(cd[0][h]), bank(bkv)[:, h * 128:(h + 1) * 128], MUL, ADD), [pk(bkv)], ['Rf'])
                    op('pool', CP(Rb, Rf), ['Rf'], ['Rb'])
                    if T_ == ntp - 1:
                        dma_store(o_retp.rearrange("h d e -> d h e"), Rf, ['Rf'])
                else:
                    pass
                for hh in range(2):
                    bb = nb()
                    fns = []
                    for mi in range(4):
                        m = hh * 4 + mi
                        for kc in range(4):
                            fns.append(MM(bank(bb)[:, mi * 128:(mi + 1) * 128], wbr[:, kc, m * 128:(m + 1) * 128], retT[:, kc, :], kc == 0, kc == 3))
                    pe_group(fns, ['retT', 'wbr'], [pk(bb)])
                    op('dve', TT(crT[:, hh * 4:(hh + 1) * 4, T_ * 128:(T_ + 1) * 128], bank(bb).rearrange("p (a b) -> p a b", a=4),
                                 sgr2[sl][:, hh * 4:(hh + 1) * 4, :], MUL), [pk(bb), ('sgr', sl, hh)], [('crT', T_, hh)])

            pslo[0] = 4
            stA(0)
            for T_ in range(NT):
                stB1(T_)
                if T_ + 1 < NT:
                    stA(T_ + 1)
                stB2(T_)
            pslo[0] = 0

        s5 = {}
        S5KEYS = ['WstRe', 'WstIm', 'Toep', 'CL', 'L4', 'dbc', 'tabs']

        def alloc_s5():
            o0 = AR.off
            WstRe = AR.alloc([32, 64], BF16)
            WstIm = AR.alloc([32, 64], BF16)
            Toep = AR.alloc([16, 128], BF16)
            CLre = AR.alloc([32, 128], BF16)
            CLim = AR.alloc([32, 128], BF16)
            L4 = AR.alloc([4, 32], F32)
            dbc = AR.alloc([512], F32)
            cosT = AR.alloc([32, 32], F32)
            sinT = AR.alloc([32, 32], F32)
            rpow = AR.alloc([32, 32], F32)
            s5.update(WstRe=WstRe, WstIm=WstIm, Toep=Toep, CLre=CLre, CLim=CLim, L4=L4, dbc=dbc,
                      cosT=cosT, sinT=sinT, rpow=rpow, keep=AR.off,
                      region=AR.t[:, o0 // 4:AR.off // 4], nwords=(AR.off - o0) // 4)
            sizes = [('WstRe', 1024), ('WstIm', 1024), ('Toep', 1024), ('CLre', 2048), ('CLim', 2048), ('L4', 128),
                     ('dbc', 512), ('cosT', 1024), ('sinT', 1024), ('rpow', 1024)]
            o_ = 0
            s5['woff'] = {}
            for nm, nw in sizes:
                s5['woff'][nm] = (o_, nw)
                o_ += nw
            assert o_ == s5['nwords'], (o_, s5['nwords'])

        def s5setup():
            AR.reset(PH0_BYTES)
            alloc_s5()
            op('dve', MS(s5['region'], 0.0), [], S5KEYS)
            WstRe, WstIm, Toep, CLre, CLim, L4, dbc, cosT, sinT, rpow = (s5[k] for k in
                ('WstRe', 'WstIm', 'Toep', 'CLre', 'CLim', 'L4', 'dbc', 'cosT', 'sinT', 'rpow'))
            tC = [AR.alloc([32, 16], F32) for _ in range(4)]
            tW = AR.alloc([4, 32], F32)
            lamT = AR.alloc([256], F32)
            lamp = AR.alloc([2, 32], F32)
            dtb = AR.alloc([32], F32)
            Bre = AR.alloc([32, 16], F32)
            Bim = AR.alloc([32, 16], F32)
            Cnat = AR.alloc([4, 2, 64], F32)
            Cre = AR.alloc([32, 16], F32)
            Cim = AR.alloc([32, 16], F32)
            tA = [AR.alloc([32], F32) for _ in range(12)]
            tI = AR.alloc([32], I32)
            PW = AR.alloc([8, 2, 32], F32)
            EL = AR.alloc([4, 2, 32], F32)
            WallRe = AR.alloc([32, 128], F32)
            WallIm = AR.alloc([32, 128], F32)
            RmRe = AR.alloc([32, 128], F32)
            RmIm = AR.alloc([32, 128], F32)
            tB = [AR.alloc([16, 16], F32) for _ in range(4)]
            tmask = AR.alloc([128], F32)
            P64 = slice(0, 64)

            def ld_lam(e, s):
                e.dma_start(out=lamT[0:32, 0:64], in_=lam_re).then_inc(s, 16)
                e.dma_start(out=lamT[0:32, 64:128], in_=lam_im).then_inc(s, 16)
            S.add('sp', ld_lam, reads=[], writes=['lamT'], dma=uk(), ndma=2)
            dma_load(dtb[P64, :], log_dt.partition_broadcast(64), 'dtb')
            def ld_b(e, s_):
                for g4 in range(8):
                    e.dma_start(out=Bre[P64, g4 * 4:(g4 + 1) * 4, :], in_=b_re[g4 * 4:(g4 + 1) * 4].rearrange("g p m -> p g m")).then_inc(s_, 16)
                    e.dma_start(out=Bim[P64, g4 * 4:(g4 + 1) * 4, :], in_=b_im[g4 * 4:(g4 + 1) * 4].rearrange("g p m -> p g m")).then_inc(s_, 16)
            S.add('sp', ld_b, reads=[], writes=['Bre', 'Bim'], dma=uk(), ndma=16)

            def ld_c(e, s):
                for ti in range(4):
                    e.dma_start(out=Cnat[:, ti, 0, :], in_=c_re[ti * 128:(ti + 1) * 128, :]).then_inc(s, 16)
                    e.dma_start(out=Cnat[:, ti, 1, :], in_=c_im[ti * 128:(ti + 1) * 128, :]).then_inc(s, 16)
            S.add('sp', ld_c, reads=[], writes=['Cnat'], dma=uk(), ndma=8)
            dma_load(dbc, ssm_d.partition_broadcast(128), 'dbc')
            dma_load(tmask, k_tmask, 'tmask')

            b = nb()
            pe_group([TR(bank(b)[0:64, 0:32], lamT[0:32, 0:64], ident[0:32, 0:32]),
                      TR(bank(b)[0:64, 32:64], lamT[0:32, 64:128], ident[0:32, 0:32])],
                     ['lamT', 'ident'], [pk(b)])
            op('act', ACP(lamp[P64].rearrange("p a b -> p (a b)"), bank(b)[0:64, 0:64]), [pk(b)], ['lamp'])
            for part, Cdst, ckey in ((0, Cre, 'Cre'), (1, Cim, 'Cim')):
                b = nb()
                pe_group([TR(bank(b)[0:64, ti * 128:(ti + 1) * 128], Cnat[:, ti, part, :], ident) for ti in range(4)],
                         ['Cnat', 'ident'], [pk(b)])
                op('act', ACP(Cdst[P64].rearrange("p g m -> p (g m)"), bank(b)[0:64, :]), [pk(b)], [ckey])

            off_save = AR.off
            yield
            AR.off = off_save
            K = ['s5t']
            lre = lamp[P64, 0, :]
            lim = lamp[P64, 1, :]
            T = [t[P64] for t in tA]

            def dv(fn, r=(), w=()):
                op('dve', fn, list(r) + K, list(w) + K)

            def ac(fn, r=(), w=()):
                op('act', fn, list(r) + K, list(w) + K)

            def pw(l, part):
                return PW[P64, l + 3, part, :]

            ac(ACT(T[0], dtb[P64], AF.Exp), ['dtb', 'lamp'])
            dv(TT(T[1], lre, T[0], MUL))
            dv(TT(T[2], lim, T[0], MUL))
            ac(ACT(T[3], T[1], AF.Exp))
            ac(ACT(T[4], T[1], AF.Exp, scale=-1.0))
            ac(ACT(tW[P64, 2, :], T[1], AF.Exp, scale=4.0))
            ac(ACT(tW[P64, 3, :], T[1], AF.Exp, scale=-4.0))

            def sin_of(dst, src_ang, shift):
                dv(TS(T[5], src_ang, shift, None, ADD))
                dv(TS(tI[P64], T[5], 1.0 / TWO_PI, None, MUL))
                dv(CP(T[6], tI[P64]))
                dv(STT(T[7], T[6], -TWO_PI, T[5], MUL, ADD))
                dv(TS(T[7], T[7], 3.1415925, -3.1415925, ALU.min, ALU.max))
                ac(ACT(dst, T[7], AF.Sin))

            sin_of(T[8], T[2], 0.0)
            sin_of(T[9], T[2], math.pi / 2.0)
            dv(TT(pw(1, 0), T[3], T[9], MUL))
            dv(TT(pw(1, 1), T[3], T[8], MUL))
            dv(TT(pw(-1, 0), T[4], T[9], MUL))
            dv(STT(pw(-1, 1), T[4], -1.0, T[8], MUL, MUL))
            dv(MS(pw(0, 0), 1.0))
            dv(MS(pw(0, 1), 0.0))

            def cmul(ore, oim, are, aim, bre, bim):
                dv(TT(T[5], are, bre, MUL))
                dv(TT(T[6], aim, bim, MUL))
                dv(TT(T[7], are, bim, MUL))
                dv(TT(T[10], aim, bre, MUL))
                dv(TT(ore, T[5], T[6], SUB))
                dv(TT(oim, T[7], T[10], ADD))

            for l in (2, 3, 4):
                cmul(pw(l, 0), pw(l, 1), pw(l - 1, 0), pw(l - 1, 1), pw(1, 0), pw(1, 1))
            for l in (-2, -3):
                cmul(pw(l, 0), pw(l, 1), pw(l + 1, 0), pw(l + 1, 1), pw(-1, 0), pw(-1, 1))
            K1 = list(K)

            def dv(fn, r=(), w=()):
                op('dve', fn, list(r) + ['s5t', 's5k'], list(w) + ['s5k'])

            def cmul(ore, oim, are, aim, bre, bim):
                dv(TT(T[5], are, bre, MUL))
                dv(TT(T[6], aim, bim, MUL))
                dv(TT(T[7], are, bim, MUL))
                dv(TT(T[10], aim, bre, MUL))
                dv(TT(ore, T[5], T[6], SUB))
                dv(TT(oim, T[7], T[10], ADD))
            dv(TS(T[0], pw(1, 0), -1.0, None, ADD))
            dv(TT(T[1], lre, lre, MUL))
            dv(TT(T[2], lim, lim, MUL))
            dv(TT(T[1], T[1], T[2], ADD))
            dv(lambda e: e.reciprocal(out=T[1], in_=T[1]))
            dv(TT(T[2], T[0], lre, MUL))
            dv(TT(T[3], pw(1, 1), lim, MUL))
            dv(TT(T[2], T[2], T[3], ADD))
            dv(TT(T[8], T[2], T[1], MUL))
            dv(TT(T[2], pw(1, 1), lre, MUL))
            dv(TT(T[3], T[0], lim, MUL))
            dv(TT(T[2], T[2], T[3], SUB))
            dv(TT(T[9], T[2], T[1], MUL))
            for l in range(4):
                cmul(EL[P64, l, 0, :], EL[P64, l, 1, :], pw(l, 0), pw(l, 1), T[8], T[9])
            dv(CP(L4[P64, 0, :], pw(4, 0)), w=['L4'])
            dv(CP(L4[P64, 1, :], pw(4, 1)), w=['L4'])
            dv(TS(L4[P64, 2, :], pw(4, 1), -1.0, None, MUL), w=['L4'])

            tP = [AR.alloc([32], F32)[P64] for _ in range(4)]

            def dv(fn, r=(), w=()):
                op('pool', fn, list(r) + ['s5t', 's5p'], list(w) + ['s5p'])

            def cmul(ore, oim, are, aim, bre, bim):
                dv(TT(tP[0], are, bre, MUL))
                dv(TT(tP[1], aim, bim, MUL))
                dv(TT(tP[2], are, bim, MUL))
                dv(TT(tP[3], aim, bre, MUL))
                dv(TT(ore, tP[0], tP[1], SUB))
                dv(TT(oim, tP[2], tP[3], ADD))
            c64 = cosT[P64]
            s64 = sinT[P64]
            dv(MS(rpow[P64, :, 0:1], 0.0), w=['tabs'])
            dv(CP(rpow[P64, :, 1:32], tW[P64, 2, :].unsqueeze(2).to_broadcast([64, 32, 31])), w=['tabs'])
            dv(MS(c64[:, :, 0:1], 1.0), w=['tabs'])
            dv(MS(s64[:, :, 0:1], 0.0), w=['tabs'])
            dv(TT(c64[:, :, 1], pw(4, 0), tW[P64, 3, :], MUL), w=['tabs'])
            dv(TT(s64[:, :, 1], pw(4, 1), tW[P64, 3, :], MUL), w=['tabs'])
            dv(CP(tW[P64, 0, :], c64[:, :, 1]))
            dv(CP(tW[P64, 1, :], s64[:, :, 1]))
            tc = [t[P64] for t in tC]
            for k in (2, 4, 8, 16):
                cmul(tW[P64, 0, :], tW[P64, 1, :], c64[:, :, k // 2], s64[:, :, k // 2], c64[:, :, k // 2], s64[:, :, k // 2])
                wr = tW[P64, 0, :].unsqueeze(2).to_broadcast([64, 32, k])
                wi = tW[P64, 1, :].unsqueeze(2).to_broadcast([64, 32, k])
                a_ = [t[:, :, 0:k] for t in tc]
                dv(TT(a_[0], c64[:, :, 0:k], wr, MUL))
                dv(TT(a_[1], s64[:, :, 0:k], wi, MUL))
                dv(TT(a_[2], c64[:, :, 0:k], wi, MUL))
                dv(TT(a_[3], s64[:, :, 0:k], wr, MUL))
                dv(TT(c64[:, :, k:2 * k], a_[0], a_[1], SUB), w=['tabs'])
                dv(TT(s64[:, :, k:2 * k], a_[2], a_[3], ADD), w=['tabs'])
            def dv(fn, r=(), w=()):
                op('dve', fn, list(r) + ['s5m'], list(w) + ['s5m'])
            dv(MS(WallRe[P64], 0.0), w=['Wall'])
            dv(MS(WallIm[P64], 0.0), w=['Wall'])
            dv(MS(RmRe[P64], 0.0), w=['Rm'])
            dv(MS(RmIm[P64], 0.0), w=['Rm'])
            dv(MS(CLre[P64], 0.0), w=['CL'])
            dv(MS(CLim[P64], 0.0), w=['CL'])

            def gsel(ap3, x):
                return ap3.rearrange("p (gp x) m -> p gp x m", x=2)[:, :, x, :]

            def bsel(ap2, x):
                return ap2.rearrange("p (gp x) -> p gp x", x=2)[:, :, x].unsqueeze(2).to_broadcast([64, 16, 16])

            def dsel(W, x, j):
                return W.rearrange("p (gp x) c -> p gp x c", x=2)[:, :, x, j * 32 + x * 16: j * 32 + x * 16 + 16]

            NLANE = 2
            lanes = [[t[P64] for t in tB]] + [[AR.alloc([16, 16], F32)[P64] for _ in range(4)] for _ in range(NLANE - 1)]
            lane_ctr = [0]
            wall_keys, rm_keys, cl_keys = [], [], []

            def cprod(ar, ai, br, bi, dre, dim_, rkeys, okey, neg_im):
                ln = lane_ctr[0] % NLANE
                lane_ctr[0] += 1
                t_ = lanes[ln]
                tk = [('tbk', ln, i_) for i_ in range(4)]
                rk_ = list(rkeys)
                op('dve', TT(t_[0], ar, br, MUL), rk_, [tk[0]])
                op('dve', TT(t_[1], ai, bi, MUL), rk_, [tk[1]])
                op('dve', TT(t_[2], ai if neg_im else ar, br if neg_im else bi, MUL), rk_, [tk[2]])
                op('dve', TT(t_[3], ar if neg_im else ai, bi if neg_im else br, MUL), rk_, [tk[3]])
                op('dve', TT(dre, t_[0], t_[1], SUB), [tk[0], tk[1], okey[0]], [okey[1]])
                if neg_im:
                    op('dve', STT(dim_, t_[2], -1.0, t_[3], MUL, SUB), [tk[2], tk[3], okey[0]], [okey[2]])
                else:
                    op('dve', TT(dim_, t_[2], t_[3], ADD), [tk[2], tk[3], okey[0]], [okey[2]])

            for l in range(4):
                j = 3 - l
                for x in range(2):
                    k1, k2 = ('Wall', 0, x, j), ('Wall', 1, x, j)
                    wall_keys += [k1, k2]
                    cprod(bsel(EL[P64, l, 0, :], x), bsel(EL[P64, l, 1, :], x), gsel(Bre[P64], x), gsel(Bim[P64], x),
                          dsel(WallRe[P64], x, j), dsel(WallIm[P64], x, j), ['Bre', 'Bim', 's5k'], ('Wall', k1, k2), False)
            for s_ in range(4):
                for (pwr, Rr, Ri, wk, klist) in ((s_ - 3, RmRe, RmIm, 'Rm', rm_keys), (s_ + 1, CLre, CLim, 'CL', cl_keys)):
                    for x in range(2):
                        k1, k2 = (wk, 0, x, s_), (wk, 1, x, s_)
                        klist += [k1, k2]
                        cprod(bsel(pw(pwr, 0), x), bsel(pw(pwr, 1), x), gsel(Cre[P64], x), gsel(Cim[P64], x),
                              dsel(Rr[P64], x, s_), dsel(Ri[P64], x, s_), ['Cre', 'Cim', 's5t'], (wk, k1, k2), True)
            for part, Wl, Wd, wkey in ((0, WallRe, WstRe, 'WstRe'), (1, WallIm, WstIm, 'WstIm')):
                for g8 in range(4):
                    b = nb()
                    pe_group([TR(bank(b)[:, gi * 64:(gi + 1) * 64], Wl[P64, g8 * 8 + gi, :], ident[0:64, 0:64]) for gi in range(8)],
                             ['Wall', 'ident'] + K + wall_keys, [pk(b)])
                    op('act', ACP(Wd[:, g8 * 8:(g8 + 1) * 8, :].rearrange("p g q -> p (g q)"), bank(b)), [pk(b)], [wkey])
            for gq in range(4):
                b = nb()
                fns = []
                for gi in range(4):
                    gp = gq * 4 + gi
                    seq = [(WallRe, RmRe, 2 * gp), (WallIm, RmIm, 2 * gp), (WallRe, RmRe, 2 * gp + 1), (WallIm, RmIm, 2 * gp + 1)]
                    for si, (Wl, Rm, g) in enumerate(seq):
                        fns.append(MM(bank(b)[:, gi * 128:(gi + 1) * 128], Wl[P64, g, :], Rm[P64, g, :], si == 0, si == 3))
                pe_group(fns, ['Wall', 'Rm'] + K + wall_keys + rm_keys, [pk(b)])
                op('dve', TT(Toep[:, gq * 4:(gq + 1) * 4, :], bank(b).rearrange("p (a c) -> p a c", a=4),
                             tmask.unsqueeze(1).to_broadcast([128, 4, 128]), MUL), [pk(b), 'tmask'], ['Toep'])
            s5['dram'] = nc.dram_tensor("s5d", [128, s5['nwords']], F32, kind="Internal").ap()
            dma_store(s5['dram'], s5['region'], S5KEYS + cl_keys)

        def passS():
            AR.reset(crT_box['keep'])
            crT = crT_box['crT']
            sd = s5['dram']
            wo = s5['woff']
            WstRe = AR.alloc([32, 128], BF16)
            WstIm = AR.alloc([32, 128], BF16)
            Toep = AR.alloc([16, 128], BF16)
            CLre = AR.alloc([32, 128], BF16)
            CLim = AR.alloc([32, 128], BF16)
            L4 = AR.alloc([4, 16], F32)
            dbc = AR.alloc([512], F32)
            cosT = AR.alloc([16, 32], F32)
            sinT = AR.alloc([16, 32], F32)
            rpow = AR.alloc([16, 32], F32)
            for t_ in (WstRe, WstIm, CLre, CLim):
                op('pool', MS(t_, 0.0), [], ['s5z'])

            def wv(t_):
                a_, b_ = t_.shape[1], t_.shape[2]
                return t_.rearrange("p a b -> p (a b)").bitcast(F32).rearrange("p (a b) -> p a b", a=a_)

            def park(nm, rows=slice(0, 128)):
                o_, n_ = wo[nm]
                return sd[rows, o_:o_ + n_]

            def ld_s5(e, s_):
                n = 0
                for Wt, nm in ((WstRe, 'WstRe'), (WstIm, 'WstIm')):
                    src = park(nm).rearrange("p (g q) -> p g q", g=32)
                    for gh in range(2):
                        e.dma_start(out=wv(Wt)[:, gh * 16:(gh + 1) * 16, gh * 32:(gh + 1) * 32],
                                    in_=src[:, gh * 16:(gh + 1) * 16, :]).then_inc(s_, 16); n += 1
                e.dma_start(out=wv(Toep).rearrange("p a b -> p (a b)"), in_=park('Toep')).then_inc(s_, 16); n += 1
                for Ct, nm in ((CLre, 'CLre'), (CLim, 'CLim')):
                    src = park(nm, slice(0, 64)).rearrange("p (g q) -> p g q", g=32)
                    for gh in range(2):
                        e.dma_start(out=wv(Ct)[64 * gh:64 * gh + 64, gh * 16:(gh + 1) * 16, :],
                                    in_=src[:, gh * 16:(gh + 1) * 16, :]).then_inc(s_, 16); n += 1
                srcL = park('L4', slice(0, 64)).rearrange("p (c g) -> p c g", c=4)
                for gh in range(2):
                    e.dma_start(out=L4[64 * gh:64 * gh + 64], in_=srcL[:, :, gh * 16:(gh + 1) * 16]).then_inc(s_, 16); n += 1
                e.dma_start(out=dbc, in_=park('dbc')).then_inc(s_, 16); n += 1
                for Tt, nm in ((cosT, 'cosT'), (sinT, 'sinT'), (rpow, 'rpow')):
                    src = park(nm, slice(0, 64)).rearrange("p (g n) -> p g n", g=32)
                    for gh in range(2):
                        e.dma_start(out=Tt[64 * gh:64 * gh + 64], in_=src[:, gh * 16:(gh + 1) * 16, :]).then_inc(s_, 16); n += 1
                assert n == NLD, n
            NLD = 4 + 1 + 4 + 2 + 1 + 6
            S.add('sp', ld_s5, reads=['s5z'], writes=list(S5KEYS), dma=uk(), ndma=NLD, nbytes=6 * 1024 * 1024)

            winS = AR.alloc_tail([8, 1536], BF16)
            wglu = AR.alloc_tail([4, 512], BF16)
            wbs = AR.alloc([4, D], BF16)
            wout = AR.alloc([8, D], BF16)
            modS = [AR.alloc([D], F32) for _ in range(3)]
            xs2 = [AR.alloc([D], F32) for _ in range(3)]
            battn = AR.alloc([D], F32)
            ytm = AR.alloc([512], F32)
            e1_2 = [AR.alloc([512], F32) for _ in range(2)]
            hb2 = [AR.alloc([D], BF16) for _ in range(2)]
            hT2 = [AR.alloc([8, 128], BF16) for _ in range(2)]
            UT2 = [AR.alloc([512], BF16) for _ in range(2)]
            Dsb2 = [AR.alloc([2, 16, 32], F32) for _ in range(2)]
            HQ = AR.alloc([2, 2, 16, 32], F32)
            Hs = HQ[:, 0]
            Qs = HQ[:, 1]
            car = AR.alloc([2, 16], F32)
            Sprev = AR.alloc([2, 16, 32], BF16)
            tt = [AR.alloc([2, 16], F32) for _ in range(2)]
            e2 = AR.alloc([512], F32)
            sgl = AR.alloc([4, 128], BF16)
            ssmg = AR.alloc([4, 128], BF16)
            sgs2 = [AR.alloc([8, 128], BF16) for _ in range(2)]
            mrg = AR.alloc([8, 128], BF16)
            Sfin = AR.alloc([2, 128], F32)
            mrg32 = e2.rearrange("p (a b) -> p a b", a=4)

            cast_load_rows(wbs, w_bs, 4, 'wbs')
            cast_load_rows(wout, w_out, 8, 'wout')
            op('dve', MS(car, 0.0), [], ['car'])
            cos2 = cosT.rearrange("p g n -> p (g n)").unsqueeze(1).to_broadcast([128, 2, 512])
            sin1 = sinT.rearrange("p g n -> p (g n)")
            rp1 = rpow.rearrange("p g n -> p (g n)")
            Lre2 = L4[:, 0, :]
            Lim_ = L4[:, 1, :]
            nLim = L4[:, 2, :]

            def load_x(T_):
                s3 = T_ % 3
                dma_load(xs2[s3], xin[T_], ('xs', s3), 'xS%d' % s3)

            def stApre(T_):
                ty = type_of(T_)
                sl = T_ % 2
                if T_ == ntp:
                    load_mod(modS, 1, [0, 1])
                xkey = ('xs', T_ % 3)
                if T_ + 1 < NT:
                    load_x(T_ + 1)
                hb = hb2[sl]
                hT = hT2[sl]
                Dsb = Dsb2[sl]
                frontend(xs2[T_ % 3], xkey, modS[1], modS[0], [('modA', 1)], [('modA', 0)], hb, hT, None, hbk=('hb', sl), hTk=('hT', sl))
                bu = nb()
                pe_group([MM(bank(bu), hT[:, k, :], winS[:, k, 0:512], k == 0, k == 7) for k in range(8)], [('hT', sl), 'winS0'], [pk(bu)])
                op('dve', TT(e1_2[sl], bank(bu), dbc, MUL), [pk(bu), 'dbc'], [('e1', sl)])
                op('dve', lambda e, bu=bu: e.transpose(out=ytm, in_=bank(bu)), [pk(bu)], ['ytm'])
                op('act', ACP(UT2[sl], ytm), ['ytm'], [('UT', sl)])
                for hh in range(2):
                    bb = nb()
                    fns = []
                    for mi in range(4):
                        m = hh * 4 + mi
                        for k in range(8):
                            fns.append(MM(bank(bb)[:, mi * 128:(mi + 1) * 128], winS[:, k, 512 + m * 128:512 + (m + 1) * 128], hT[:, k, :], k == 0, k == 7))
                    pe_group(fns, [('hT', sl), 'winS1'], [pk(bb)])
                    op('act', ACT(sgs2[sl][:, hh * 4:(hh + 1) * 4, :].rearrange("p a b -> p (a b)"), bank(bb), AF.Sigmoid), [pk(bb)], [('sgs', sl, hh)])
                for part, Wst, wkey in ((0, WstRe, 'WstRe'), (1, WstIm, 'WstIm')):
                    bd = nb()
                    fns = []
                    for gl in range(16):
                        for gh in range(2):
                            g = gh * 16 + gl
                            fns.append(MM(bank(bd)[:, gl * 32:(gl + 1) * 32], Wst[:, g, :], UT2[sl][:, (g // 2) * 32:(g // 2 + 1) * 32], gh == 0, gh == 1))
                    pe_group(fns, [('UT', sl), wkey], [pk(bd)])
                    op('act', ACP(Dsb[:, part].rearrange("p g n -> p (g n)"), bank(bd)), [pk(bd)], [('Dsb', sl, part)])

            def stRec(T_):
                ty = type_of(T_)
                sl = T_ % 2
                dkeys = [('Dsb', sl, 0), ('Dsb', sl, 1)]
                Dsb = Dsb2[sl]
                t0 = tt[0]
                t1 = tt[1]
                if ty == 0:
                    D2 = Dsb.rearrange("p c g n -> p c (g n)")
                    H2 = Hs.rearrange("p c g n -> p c (g n)")
                    Q2 = Qs.rearrange("p c g n -> p c (g n)")
                    op('dve', TT(t0, car, Lre2.unsqueeze(1).to_broadcast([128, 2, 16]), MUL), ['car', 'L4'], ['t0'])
                    op('dve', TT(t1[:, 0, :], car[:, 1, :], nLim, MUL), ['car', 'L4'], ['t1'])
                    op('dve', TT(t1[:, 1, :], car[:, 0, :], Lim_, MUL), ['car', 'L4'], ['t1'])
                    op('dve', TT(t0, t0, t1, ADD), ['t1'], ['t0'])
                    op('dve', TT(Dsb[:, :, :, 0], Dsb[:, :, :, 0], t0, ADD), ['t0'], dkeys)
                    op('dve', TT(H2, D2, cos2, MUL), dkeys + ['tabs'], ['Hs'])
                    op('pool', TT(Q2[:, 0, :], D2[:, 1, :], sin1, MUL), dkeys + ['tabs'], [('Qs', 0)])
                    op('pool', TT(Q2[:, 1, :], D2[:, 0, :], sin1, MUL), dkeys + ['tabs'], [('Qs', 1)])
                    op('dve', TT(H2[:, 0, :], H2[:, 0, :], Q2[:, 0, :], ADD), [('Qs', 0)], ['Hs'])
                    op('dve', TT(H2[:, 1, :], H2[:, 1, :], Q2[:, 1, :], SUB), [('Qs', 1)], ['Hs'])
                    for c_ in range(2):
                        op('dve', withcost((lambda c_: lambda e: e.tensor_tensor_scan(out=Q2[:, c_, :], data0=rp1, data1=H2[:, c_, :], initial=0.0,
                                                                                      op0=MUL, op1=ADD))(c_), 1.3), ['Hs', 'tabs'], [('Qs', c_)])
                    op('dve', TT(D2, Q2, cos2, MUL), [('Qs', 0), ('Qs', 1), 'tabs'], dkeys)
                    op('pool', TT(H2[:, 0, :], Q2[:, 1, :], sin1, MUL), [('Qs', 1), 'tabs'], ['Hs'])
                    op('pool', TT(H2[:, 1, :], Q2[:, 0, :], sin1, MUL), [('Qs', 0), 'tabs'], ['Hs'])
                    op('dve', TT(D2[:, 0, :], D2[:, 0, :], H2[:, 0, :], SUB), ['Hs'], dkeys)
                    op('dve', TT(D2[:, 1, :], D2[:, 1, :], H2[:, 1, :], ADD), ['Hs'], dkeys)
                    op('act', ACP(Sprev[:, :, :, 1:32], Dsb[:, :, :, 0:31]), dkeys, ['Sprev'])
                    op('act', ACP(Sprev[:, :, :, 0], car), ['car'], ['Sprev'])
                    op('dve', CP(car, Dsb[:, :, :, 31]), dkeys, ['car'])
                    if T_ == ntp - 1:
                        bf_ = nb()
                        pe_group([TR(bank(bf_)[0:16, part * 128:(part + 1) * 128], car[:, part, :], ident) for part in range(2)],
                                 ['car', 'ident'], [pk(bf_)])
                        op('act', ACP(Sfin[0:16].rearrange("p a b -> p (a b)"), bank(bf_)[0:16, 0:256]), [pk(bf_)], ['Sfin'])
                        for part, dst in ((0, o_srep), (1, o_simp)):
                            dma_store(dst.rearrange("(gh gl) p -> gl gh p", gh=2), Sfin[0:16, part, :].rearrange("g (gh p) -> g gh p", gh=2), ['Sfin'])
                else:
                    assert sl == 0
                    dfl = Dsb2[1].rearrange("p c g n -> p (c g n)")
                    SiT = dfl[:, 0:512].rearrange("p (c g b) -> p c g b", c=2, g=16)
                    S1T = dfl[:, 512:1024].rearrange("p (c g b) -> p c g b", c=2, g=16)
                    stg = HQ.rearrange("p a c g n -> p (a c g n)")
                    stg3 = stg[0:16, :].rearrange("b (gl gh p) -> b gl gh p", gl=16, gh=2)
                    qk = ['Hs', ('Qs', 0), ('Qs', 1)]
                    for part, src in ((0, ssre), (1, ssim)):
                        def ld_st(e, s_, src=src):
                            sv = src.rearrange("b (gh gl p) -> b gh gl p", gh=2, gl=16)
                            for gh in range(2):
                                e.dma_start(out=stg3[:, :, gh, :], in_=sv[:, gh]).then_inc(s_, 16)
                        S.add('sp', ld_st, reads=[], writes=qk, dma=uk(), ndma=2)
                        b = nb()
                        pe_group([TR(bank(b)[:, gl * 16:(gl + 1) * 16], stg[0:16, gl * 128:(gl + 1) * 128], ident[0:16, 0:16]) for gl in range(16)],
                                 qk + ['ident'], [pk(b)])
                        op('act', ACP(SiT[:, part], bank(b)[:, 0:256].rearrange("p (g b) -> p g b", g=16)), [pk(b)], [('Dsb', 1, 0)])
                    Dv = Dsb.rearrange("p c g (b h) -> p c g b h", h=2)
                    lre_b = Lre2.unsqueeze(2).to_broadcast([128, 16, 16])
                    lim_b = Lim_.unsqueeze(2).to_broadcast([128, 16, 16])
                    w0 = e2[:, 0:256].rearrange("p (g b) -> p g b", g=16)
                    w1 = ytm[:, 0:256].rearrange("p (g b) -> p g b", g=16)

                    def cstep(Sa, Sb, hsel, ka, kb):
                        op('dve', TT(w0, Sa[:, 0], lre_b, MUL), [ka, 'L4'], ['e2'])
                        op('dve', TT(w1, Sa[:, 1], lim_b, MUL), [ka, 'L4'], ['ytm'])
                        op('dve', TT(w0, w0, w1, SUB), ['ytm'], ['e2'])
                        op('dve', TT(Sb[:, 0], w0, Dv[:, 0, :, :, hsel], ADD), ['e2'] + dkeys, [kb])
                        op('dve', TT(w0, Sa[:, 1], lre_b, MUL), [ka, 'L4'], ['e2'])
                        op('dve', TT(w1, Sa[:, 0], lim_b, MUL), [ka, 'L4'], ['ytm'])
                        op('dve', TT(w0, w0, w1, ADD), ['ytm'], ['e2'])
                        op('dve', TT(Sb[:, 1], w0, Dv[:, 1, :, :, hsel], ADD), ['e2'] + dkeys, [kb])
                    Spv = Sprev.rearrange("p c g (b h) -> p c g b h", h=2)
                    op('dve', CP(Spv[:, :, :, :, 0], SiT), [('Dsb', 1, 0)], ['Sprev'])
                    cstep(SiT, S1T, 0, ('Dsb', 1, 0), ('Dsb', 1, 1))
                    op('dve', CP(Spv[:, :, :, :, 1], S1T), [('Dsb', 1, 1)], ['Sprev'])
                    cstep(S1T, SiT, 1, ('Dsb', 1, 1), ('Dsb', 1, 0))
                    for part, dst in ((0, o_sres), (1, o_sims)):
                        for gq in range(4):
                            bo_ = nb()
                            pe_group([TR(bank(bo_)[0:16, gi * 128:(gi + 1) * 128], SiT[:, part, gq * 4 + gi, :], ident) for gi in range(4)],
                                     [('Dsb', 1, 0), 'ident'], [pk(bo_)])
                            op('act', ACP(stg[0:16, gq * 512:(gq + 1) * 512], bank(bo_)[0:16, :]), [pk(bo_)], qk)
                        def st_st(e, s_, dst=dst):
                            dv_ = dst.rearrange("b (gh gl p) -> b gh gl p", gh=2, gl=16)
                            for gh in range(2):
                                e.dma_start(out=dv_[:, gh], in_=stg3[:, :, gh, :]).then_inc(s_, 16)
                        S.add('sp', st_st, reads=qk, writes=[], dma=uk(), ndma=2)

            def stB1(T_):
                sl = T_ % 2
                by = 0
                fns = []
                for gp in range(16):
                    o_ = bank(by)[:, gp * 32:(gp + 1) * 32]
                    fns.append(MM(o_, Toep[:, gp, :], UT2[sl][:, gp * 32:(gp + 1) * 32], True, False))
                    for x in range(2):
                        g = 2 * gp + x
                        fns.append(MM(o_, CLre[:, g, :], Sprev[:, 0, g % 16, :], False, False))
                        fns.append(MM(o_, CLim[:, g, :], Sprev[:, 1, g % 16, :], False, x == 1))
                pe_group(fns, [('UT', sl), 'Toep', 'CL', 'Sprev'], [pk(by)])

            def stB2(T_):
                ty = type_of(T_)
                sl = T_ % 2
                xkey = ('xs', T_ % 3)
                e1 = e1_2[sl]
                ssmb = hb2[sl][:, 0:512]
                ssmT = hb2[sl][:, 512:1024].rearrange("p (a b) -> p a b", a=4)
                hbk_ = ('hb', sl)
                if T_ == ntp:
                    load_mod(modS[2:3], 1, [2])
                op('dve', lambda e: e.transpose(out=ytm, in_=bank(0)), [pk(0)], ['ytm'])
                op('pool', TT(ytm, ytm, e1, ADD), [('e1', sl)], ['ytm'])
                op('act', ACT(e1, ytm, AF.Square), ['ytm'], [('e1', sl)])
                op('act', withcost(lambda e, e1=e1: e.activation(out=e1, in_=e1, func=AF.Identity, scale=0.044715, bias=1.0), ('tt', 512)), [], [('e1', sl)])
                op('dve', TT(e1, e1, ytm, MUL), ['ytm'], [('e1', sl)])
                op('act', ACT(e2, e1, AF.Sigmoid, scale=1.5957691216057308), [('e1', sl)], ['e2'])
                op('dve', TT(ssmb, ytm, e2, MUL), ['ytm', 'e2'], [hbk_])
                bst = nb()
                pe_group([TR(bankbf(bst)[:, c * 128:(c + 1) * 128], ssmb[:, c * 128:(c + 1) * 128], identb) for c in range(4)], [hbk_, 'identb'], [pk(bst)])
                op('act', ACP(ssmT.rearrange("p a b -> p (a b)"), bankbf(bst)[:, 0:512]), [pk(bst)], [hbk_])
                bgl = nb()
                fns = []
                for m in range(4):
                    for kc in range(4):
                        fns.append(MM(bank(bgl)[:, m * 128:(m + 1) * 128], wglu[:, kc, m * 128:(m + 1) * 128], ssmT[:, kc, :], kc == 0, kc == 3))
                pe_group(fns, [hbk_, 'wglu'], [pk(bgl)])
                op('act', ACT(sgl.rearrange("p a b -> p (a b)"), bank(bgl), AF.Sigmoid), [pk(bgl)], ['sgl'])
                op('dve', TT(ssmg, ssmT, sgl, MUL), [hbk_, 'sgl'], ['ssmg'])
                for hh in range(2):
                    bb = nb()
                    fns = []
                    for mi in range(4):
                        m = hh * 4 + mi
                        for kc in range(4):
                            fns.append(MM(bank(bb)[:, mi * 128:(mi + 1) * 128], wbs[:, kc, m * 128:(m + 1) * 128], ssmg[:, kc, :], kc == 0, kc == 3))
                    pe_group(fns, ['ssmg', 'wbs'], [pk(bb)])
                    op('dve', TT(mrg32, bank(bb).rearrange("p (a b) -> p a b", a=4), sgs2[sl][:, hh * 4:(hh + 1) * 4, :], MUL), [pk(bb), ('sgs', sl, hh)], ['e2'])
                    op('dve', TT(mrg[:, hh * 4:(hh + 1) * 4, :], mrg32, crT[:, hh * 4:(hh + 1) * 4, T_ * 128:(T_ + 1) * 128], ADD), ['e2'], [('mrg', hh)])
                ba = 6
                for hf in range(2):
                    pe_group([MM(bank(ba + hf), mrg[:, k, :], wout[:, k, hf * 512:(hf + 1) * 512], k == 0, k == 7) for k in range(8)],
                             [('mrg', 0), ('mrg', 1), 'wout'], [pk(ba + hf)])
                pa = PS[:, 512 * ba:512 * (ba + 2)]
                op('dve', TT(battn, pa, modS[2], MUL), [pk(ba), pk(ba + 1), ('modA', 2)], ['battn'])
                op('pool', TT(battn, battn, xs2[T_ % 3], ADD), [xkey], ['battn'])
                dma_store(x1d[T_], battn, ['battn'], 'x1d')

            pslo[0] = 1
            pshi[0] = 6
            load_x(0)
            load_mod(modS, 0, [0, 1, 2])
            stApre(0)
            stRec(0)
            for T_ in range(NT):
                if T_ + 1 < NT:
                    stApre(T_ + 1)
                stB1(T_)
                if T_ + 1 < NT:
                    stRec(T_ + 1)
                stB2(T_)
            pslo[0] = 0
            pshi[0] = 8

        def phase2():
            AR.reset()
            wup = AR.alloc([8, 4 * D], BF16)
            wdn = AR.alloc([32, D], BF16)
            modB = [AR.alloc([D], F32) for _ in range(3)]
            gfb = AR.alloc([D], F32)
            xs3 = [AR.alloc([D], F32) for _ in range(3)]
            junkF = AR.alloc([D], F32)
            hbF = AR.alloc([D], BF16)
            hT2 = [AR.alloc([8, 128], BF16) for _ in range(3)]
            rl2 = [AR.alloc([512], BF16) for _ in range(2)]
            hidT2 = [AR.alloc([32, 128], BF16) for _ in range(3)]
            junkB = AR.alloc([D], F32)
            hbB = AR.alloc([D], BF16)
            yo = AR.alloc([D], F32)
            for c4 in range(4):
                cast_load_rows(wup[:, :, c4 * D:(c4 + 1) * D], w_up, 8, ('wup', c4), c4 * D, (c4 + 1) * D)
            for c4 in range(4):
                cast_load_rows(wdn[:, c4 * 8:(c4 + 1) * 8, :], w_down[c4 * 1024:(c4 + 1) * 1024, :], 8, ('wdn', c4))
            for w3 in range(3):
                dma_load(modB[w3], modd[0, 3 + w3], ('modB', w3))
            dma_load(gfb, nfg.partition_broadcast(128), 'gfb')

            def load_x1(T_):
                s3 = T_ % 3
                dma_load(xs3[s3], x1d[T_], ('xs', s3), 'x2l%d' % s3)

            def stageA(T_, part):
                s3 = T_ % 3
                if part == 1:
                    if T_ + 1 < NT:
                        load_x1(T_ + 1)
                    if T_ == ntp:
                        for w3 in range(2):
                            dma_load(modB[w3], modd[1, 3 + w3], ('modB', w3))
                    frontend(xs3[s3], ('xs', s3), modB[1], modB[0], [('modB', 1)], [('modB', 0)], hbF, hT2[T_ % 3], junkF,
                             jk=('junkF',), hbk='hbF', hTk=('hT', T_ % 3), part=1)
                else:
                    frontend2(hbF, hT2[T_ % 3], 'hbF', ('hT', T_ % 3))

            def stageB(T_):
                sl = T_ % 3
                s3 = T_ % 3
                hT = hT2[sl]
                hidT = hidT2[sl]
                if T_ == ntp:
                    dma_load(modB[2], modd[1, 5], ('modB', 2))
                if T_ + 1 < NT:
                    stageA(T_ + 1, 1)
                for f4 in range(8):
                    rl = rl2[f4 % 2]
                    bb = nb()
                    fns = []
                    for fi in range(4):
                        f = f4 * 4 + fi
                        for k in range(8):
                            fns.append(MM(bank(bb)[:, fi * 128:(fi + 1) * 128], wup[:, k, f * 128:(f + 1) * 128], hT[:, k, :], k == 0, k == 7))
                    pe_group(fns, [('hT', sl), ('wup', f4 // 2)], [pk(bb)])
                    op('act', ACT(rl, bank(bb), AF.Relu), [pk(bb)], [('rl', f4 % 2)])
                    op('dve', TT(hidT[:, f4 * 4:(f4 + 1) * 4, :].rearrange("p a b -> p (a b)"), rl, rl, MUL), [('rl', f4 % 2)], [('hidT', sl, f4)])
                if T_ + 1 < NT:
                    stageA(T_ + 1, 2)
                bd2 = nb2()
                for hf in range(2):
                    pe_group([MM(bank(bd2 + hf), hidT[:, f, :], wdn[:, f, hf * 512:(hf + 1) * 512], f == 0, f == 31) for f in range(32)],
                             [('hidT', sl, f4) for f4 in range(8)] + [('wdn', c4) for c4 in range(4)], [pk(bd2 + hf)])
                pd = PS[:, 512 * bd2:512 * (bd2 + 2)]
                op('dve', TT(junkB, pd, modB[2], MUL), [pk(bd2), pk(bd2 + 1), ('modB', 2)], ['junkB'])
                op('dve', TT(junkB, junkB, xs3[s3], ADD), [('xs', s3)], ['junkB'])
                op('act', ACT(hbB, junkB, AF.Square, accum_out=stat[:, 24:25]), ['junkB'], ['hbB', 'fs0'])
                op('dve', TS(stat[:, 25:26], stat[:, 24:25], 1.0 / D, EPS, MUL, ADD), ['fs0'], ['fs1'])
                op('pool', TT(stat[:, 26:27], stat[:, 25:26], nhalf[:, 0:1], ALU.pow), ['fs1', 'nhalf'], ['fs2'])
                op('dve', STT(yo, junkB, stat[:, 26:27], gfb, MUL, MUL), ['junkB', 'fs2', 'gfb'], ['yo'])
                dma_store(yout[T_], yo, ['yo'], 'yout')

            load_x1(0)
            stageA(0, 1)
            stageA(0, 2)
            for T_ in range(NT):
                stageB(T_)

        import os as _os
        _stop = int(_os.environ.get('KSTOP', '9'))
        _maxops = int(_os.environ.get('KMAXOPS', '0'))
        s5gen = s5setup()
        next(s5gen)
        phase0()
        for _ in s5gen:
            pass
        if _stop >= 1:
            S.barrier()
            passR()
        if _stop >= 3:
            S.barrier()
            passS()
        if _stop >= 4:
            S.barrier()
            phase2()
        print("n_ops", len(S.ops), "n_dma_keys", len(S.dma_keys))
        S.emit(nc)
    return nc


_CACHE = {}


def make_in_maps(inputs, ntp):
    consts, cd = host_consts(ntp)
    p2t = perm_r2t()
    f = lambda a: np.ascontiguousarray(np.asarray(a, dtype=np.float32))
    xp = f(inputs['x_prompt']); xsm = f(inputs['x_sample'])
    cp = f(inputs['c_prompt']); csm = f(inputs['c_sample'])
    maps = []
    shared = {
        'w_ada': f(inputs['w_ada'][0]), 'b_ada': f(inputs['b_ada'][0]).reshape(1, -1),
        'norm1_g': f(inputs['norm1_g'][0]).reshape(1, -1), 'norm2_g': f(inputs['norm2_g'][0]).reshape(1, -1),
        'norm_f_g': f(inputs['norm_f_g']).reshape(1, -1), 'w_in': f(inputs['w_in'][0]),
        'lam_re': f(inputs['ssm_lambda_re'][0]), 'lam_im': f(inputs['ssm_lambda_im'][0]),
        'log_dt': f(inputs['ssm_log_dt'][0]).reshape(1, 32),
        'b_re': f(inputs['ssm_b_re'][0]), 'b_im': f(inputs['ssm_b_im'][0]),
        'c_re': f(inputs['ssm_c_re'][0]).reshape(512, 64), 'c_im': f(inputs['ssm_c_im'][0]).reshape(512, 64),
        'ssm_d': f(inputs['ssm_d'][0]).reshape(1, 512),
        'w_glu': f(inputs['w_glu'][0]), 'w_br': f(inputs['w_br'][0]), 'w_bs': f(inputs['w_bs'][0]),
        'w_out': f(inputs['w_out'][0]), 'w_up': f(inputs['w_up'][0]), 'w_down': f(inputs['w_down'][0]),
        'k_ident': consts['ident'], 'k_rot': consts['rot'],
        'k_maskT': consts['maskT'].reshape(2, 128, 512), 'k_qin': consts['qin'].reshape(2, 128, 512),
        'k_kout': consts['kout'], 'k_seqf': consts['seqf'].reshape(128, 2048), 'k_seqp': consts['seqp'],
        'k_tmask': consts['tmask'],
    }
    for c in range(NCORE):
        xt = np.concatenate([xp[c, :ntp * 128].reshape(ntp, 128, D), xsm[16 * c:16 * c + 16].reshape(1, 128, D)], axis=0)
        xt = np.ascontiguousarray(xt[:, p2t, :])
        ce = np.stack([np.broadcast_to(cp[c], (128, D)), np.repeat(csm[16 * c:16 * c + 16], 8, axis=0)[p2t]])
        m = dict(shared)
        m['xin'] = xt
        m['cexp'] = np.ascontiguousarray(ce)
        m['sret'] = f(inputs['state_ret'][0, 16 * c:16 * c + 16])
        m['ssre'] = f(inputs['state_ssm_re'][0, 16 * c:16 * c + 16]).reshape(16, 2048)
        m['ssim'] = f(inputs['state_ssm_im'][0, 16 * c:16 * c + 16]).reshape(16, 2048)
        maps.append(m)
    return maps, cd


def assemble(results, ntp):
    p2t = perm_r2t()
    inv = np.argsort(p2t)
    B = NCORE
    yp = np.zeros((B, ntp * 128, D), np.float32)
    ys = np.zeros((128, 8, D), np.float32)
    retp = np.zeros((1, B, 4, 128, 128), np.float32)
    srep = np.zeros((1, B, 32, 64), np.float32)
    simp = np.zeros((1, B, 32, 64), np.float32)
    rets = np.zeros((1, 128, 4, 128, 128), np.float32)
    sres = np.zeros((1, 128, 32, 64), np.float32)
    sims = np.zeros((1, 128, 32, 64), np.float32)
    for c in range(B):
        r = results[c]
        y = np.asarray(r['yout'])[:, inv, :]
        yp[c] = y[:ntp].reshape(ntp * 128, D)
        ys[16 * c:16 * c + 16] = y[ntp].reshape(16, 8, D)
        retp[0, c] = r['o_retp']
        srep[0, c] = r['o_srep']
        simp[0, c] = r['o_simp']
        rets[0, 16 * c:16 * c + 16] = r['o_rets']
        sres[0, 16 * c:16 * c + 16] = np.asarray(r['o_sres']).reshape(16, 32, 64)
        sims[0, 16 * c:16 * c + 16] = np.asarray(r['o_sims']).reshape(16, 32, 64)
    return (yp, ys, retp, srep, simp, rets, sres, sims)


def kernel(**inputs):
    ntp = 16
    maps, cd = make_in_maps(inputs, ntp)
    if ntp not in _CACHE:
        _CACHE[ntp] = build_program(ntp, cd)
    nc = _CACHE[ntp]
    res = run_bass_kernel_spmd(nc, maps, core_ids=list(range(NCORE)))
    return assemble(res.results, ntp)
```

```python
import math
from contextlib import ExitStack
import numpy as np
import concourse.bass as bass
import concourse.mybir as mybir
from concourse.bass_utils import run_bass_kernel_spmd

F32 = mybir.dt.float32
BF16 = mybir.dt.bfloat16
I32 = mybir.dt.int32
AF = mybir.ActivationFunctionType
ALU = mybir.AluOpType

ENGS = ['pe', 'act', 'dve', 'pool', 'sp']
SYNC_LAT = 0.35
SLACK = 1.0
D = 1024
NCORE = 8
EPS = 1e-6
PAST_LEN = 16384
TWO_PI = 2.0 * math.pi


class Sched:
    def __init__(self):
        self.ops = []
        self.last_w = {}
        self.readers = {}
        self.phase = 0
        self.last_dma_on_key = {}
        self.last_q = {}
        self.dma_keys = []

    def barrier(self):
        self.phase += 1
        self.last_w = {}
        self.readers = {}

    def add(self, eng, fn, reads=(), writes=(), dma=None, ndma=1, cost=None, lat=None, nbytes=None):
        idx = len(self.ops)
        deps = set()
        for k in reads:
            if k in self.last_w:
                deps.add(self.last_w[k])
        for k in writes:
            if k in self.last_w:
                deps.add(self.last_w[k])
            deps.update(self.readers.get(k, ()))
        order = set()
        if dma is not None:
            if dma not in self.last_dma_on_key:
                self.dma_keys.append(dma)
            else:
                order.add(self.last_dma_on_key[dma])
            self.last_dma_on_key[dma] = idx
        for k in writes:
            self.last_w[k] = idx
            self.readers[k] = []
        for k in reads:
            if k not in writes:
                self.readers.setdefault(k, []).append(idx)
        if cost is None:
            cost = getattr(fn, 'cost', None)
        if cost is None:
            cost = {'pe': 0.5, 'act': 0.7, 'dve': 0.7, 'pool': 1.3, 'sp': 0.05}[eng]
        if dma is not None:
            nb_ = nbytes if nbytes else 65536 * ndma
            cost = max(0.05, nb_ / (230e3 if eng == 'sp' else 170e3))
            if lat is None:
                lat = 2.5
        self.ops.append(dict(eng=eng, fn=fn, deps=deps, ord=order, dma=dma, ndma=ndma, cost=cost,
                             lat=(lat if lat is not None else 0.0), phase=self.phase))
        return idx

    def schedule(self):
        import os
        if os.environ.get('KNOSCHED'):
            return list(range(len(self.ops)))
        ops = self.ops
        order = []
        fin = {}
        tfree = {e: 0.0 for e in ENGS}
        nph = self.phase + 1
        for ph in range(nph):
            idxs = [i for i, o in enumerate(ops) if o['phase'] == ph]
            if not idxs:
                continue
            t0 = max([tfree[e] for e in ENGS] + [fin[i] for i in order[-200:]] + [0.0])
            for e in ENGS:
                tfree[e] = t0
            inph = set(idxs)
            npred = {}
            succ = {}
            for i in idxs:
                ps = [d for d in (ops[i]['deps'] | ops[i]['ord']) if d in inph]
                npred[i] = len(ps)
                for d in ps:
                    succ.setdefault(d, []).append(i)
            tail = {}
            for i in reversed(idxs):
                t_ = 0.0
                for j in succ.get(i, ()):
                    if tail[j] > t_:
                        t_ = tail[j]
                tail[i] = t_ + ops[i]['cost'] + ops[i]['lat'] + SYNC_LAT
            ready = [i for i in idxs if npred[i] == 0]
            while ready:
                sts = {}
                for i in ready:
                    o = ops[i]
                    st = tfree[o['eng']]
                    for d in o['deps']:
                        if d in fin and fin[d] + SYNC_LAT > st:
                            st = fin[d] + SYNC_LAT
                    sts[i] = st
                mn = min(sts.values())
                best = None
                for i in ready:
                    if sts[i] <= mn + SLACK:
                        if best is None or tail[i] > tail[best] + 1e-9 or (abs(tail[i] - tail[best]) <= 1e-9 and i < best):
                            best = i
                bt = sts[best]
                o = ops[best]
                ready.remove(best)
                order.append(best)
                tfree[o['eng']] = bt + o['cost']
                fin[best] = bt + o['cost'] + o['lat']
                for j in succ.get(best, ()):
                    npred[j] -= 1
                    if npred[j] == 0:
                        ready.append(j)
        assert len(order) == len(ops)
        self.est_total = max(fin.values()) if fin else 0.0
        return order

    def emit(self, nc):
        order = self.schedule()
        ops = self.ops
        cnt = {e: 0 for e in ENGS}
        dcnt = {}
        tok = {}
        for i in order:
            o = ops[i]
            if o['dma'] is None:
                cnt[o['eng']] += 1
                tok[i] = ('e', o['eng'], cnt[o['eng']])
            else:
                dcnt[o['dma']] = dcnt.get(o['dma'], 0) + 16 * o['ndma']
                tok[i] = ('d', o['dma'], dcnt[o['dma']])
        nph = self.phase + 1
        ph_end = []
        c2 = {e: 0 for e in ENGS}
        d2 = {}
        byphase = {ph: [] for ph in range(nph)}
        for i in order:
            byphase[ops[i]['phase']].append(i)
        for ph in range(nph):
            for i in byphase[ph]:
                t = tok[i]
                if t[0] == 'e':
                    c2[t[1]] = t[2]
                else:
                    d2[t[1]] = t[2]
            ph_end.append(([('e', e, c2[e]) for e in ENGS if c2[e] > 0] + [('d', k, v) for k, v in d2.items()]))
        with ExitStack() as es:
            esem = {e: es.enter_context(nc.semaphore('s_' + e)) for e in ENGS}
            dsem = {k: es.enter_context(nc.semaphore('d_%d' % i)) for i, k in enumerate(self.dma_keys)}
            block = es.enter_context(nc.Block())
            handles = {'pe': block.tensor, 'act': block.scalar, 'dve': block.vector,
                       'pool': block.gpsimd, 'sp': block.sync}

            def make(ename):
                def body(eng):
                    known = {}
                    cur_phase = 0

                    def wait_for(toks):
                        need = {}
                        for d in toks:
                            key = (d[0], d[1])
                            if d[2] > need.get(key, 0):
                                need[key] = d[2]
                        for key, val in need.items():
                            if known.get(key, 0) >= val:
                                continue
                            known[key] = val
                            if key[0] == 'e':
                                eng.wait_ge(esem[key[1]], val)
                            else:
                                eng.wait_ge(dsem[key[1]], val)

                    for i in order:
                        o = ops[i]
                        if o['eng'] != ename:
                            continue
                        if o['phase'] != cur_phase:
                            cur_phase = o['phase']
                            wait_for(ph_end[cur_phase - 1])
                        wait_for([tok[d] for d in o['deps']])
                        if o['dma'] is None:
                            ins = o['fn'](eng)
                            ins.then_inc(esem[ename], 1)
                        else:
                            o['fn'](eng, dsem[o['dma']])
                    if ename == 'sp':
                        wait_for(ph_end[-1])
                return body

            for ename in ENGS:
                handles[ename](make(ename))


class Arena:
    def __init__(self, nc, es, name, nbytes):
        self.t = es.enter_context(nc.sbuf_tensor(name, [128, nbytes // 4], F32))
        self.cap = nbytes
        self.off = 0
        self.top = nbytes

    def reset(self, off=0):
        self.off = off
        self.top = self.cap

    def alloc_tail(self, free_shape, dt):
        n = 1
        for s_ in free_shape:
            n *= s_
        esz = 4 if dt in (F32, I32) else 2
        nb = (n * esz + 31) // 32 * 32
        self.top -= nb
        save = self.off
        self.off = self.top
        ap = self.alloc(free_shape, dt)
        self.off = save
        return ap

    def alloc(self, free_shape, dt):
        n = 1
        for s in free_shape:
            n *= s
        esz = 4 if dt in (F32, I32) else 2
        nb = (n * esz + 31) // 32 * 32
        assert self.off + nb <= getattr(self, 'top', self.cap) or self.off >= getattr(self, 'top', self.cap), ("arena overflow", self.off, nb, self.cap)
        ap = self.t[:, self.off // 4:(self.off + nb) // 4]
        self.off += nb
        if dt != F32:
            ap = ap.bitcast(dt)
        ap = ap[:, 0:n]
        if len(free_shape) == 2:
            ap = ap.rearrange("p (a b) -> p a b", a=free_shape[0])
        elif len(free_shape) == 3:
            ap = ap.rearrange("p (a b c) -> p a b c", a=free_shape[0], b=free_shape[1])
        elif len(free_shape) == 4:
            ap = ap.rearrange("p (a b c d) -> p a b c d", a=free_shape[0], b=free_shape[1], c=free_shape[2])
        return ap


def perm_r2t():
    r = np.arange(128)
    return 4 * (r % 32) + r // 32


def host_consts(ntp):
    p2t = perm_r2t()
    gam = np.exp(np.log1p(-(2.0 ** (-5.0 - np.arange(4)))))
    c = {}
    c['ident'] = np.eye(128, dtype=np.float32)
    half = 64
    inv = (np.float32(10000.0) ** (-(np.arange(half, dtype=np.float32) / np.float32(half)))).astype(np.float32)
    rot = np.zeros((ntp + 1, 128, 512), np.float32)
    for T in range(ntp + 1):
        if T < ntp:
            pos = 128 * T + p2t
        else:
            pos = PAST_LEN + (p2t % 8)
        ang = (pos.astype(np.float32)[:, None] * inv[None, :]).astype(np.float32)
        co = np.cos(ang.astype(np.float64)); si = np.sin(ang.astype(np.float64))
        tq = np.concatenate([co, co, -si, si], axis=1)
        rot[T, :, 0:256] = tq
        rot[T, :, 256:512] = tq * (128.0 ** -0.5)
    c['rot'] = rot
    maskT = np.zeros((2, 128, 4, 128), np.float32)
    qin = np.zeros((2, 128, 4, 128), np.float32)
    kout = np.zeros((2, 128, 4), np.float32)
    t = p2t
    for h in range(4):
        g = gam[h]
        dlt = t[None, :] - t[:, None]
        maskT[0, :, h, :] = np.where(dlt >= 0, g ** np.maximum(dlt, 0), 0.0)
        qin[0, :, h, :] = (g ** (t + 1.0))[None, :]
        kout[0, :, h] = g ** (127.0 - t)
        b = t // 8; pos = t % 8
        dl2 = pos[None, :] - pos[:, None]
        same = (b[None, :] == b[:, None]) & (dl2 >= 0)
        maskT[1, :, h, :] = np.where(same, g ** np.maximum(dl2, 0), 0.0)
        qin[1, :, h, :] = (g ** (pos + 1.0))[None, :]
        kout[1, :, h] = g ** (7.0 - pos)
    c['maskT'] = maskT
    c['qin'] = qin
    c['kout'] = kout
    seqf = np.zeros((128, 16, 128), np.float32)
    seqp = np.zeros((128, 16), np.float32)
    for b in range(16):
        m = (t // 8 == b).astype(np.float32)
        seqf[:, b, :] = m[None, :]
        seqp[:, b] = m
    c['seqf'] = seqf
    c['seqp'] = seqp
    jj = np.arange(128) // 32
    c['tmask'] = (jj[:, None] <= jj[None, :]).astype(np.float32)
    cd = np.stack([gam ** 128.0, gam ** 8.0])
    return c, cd


CONST_SHAPES = None


def build_program(ntp, cd):
    NT = ntp + 1
    nc = bass.Bass("TRN2", target_bir_lowering=False)
    S = Sched()

    def din(name, shape):
        return nc.dram_tensor(name, list(shape), F32, kind="ExternalInput").ap()

    def dout(name, shape):
        return nc.dram_tensor(name, list(shape), F32, kind="ExternalOutput").ap()

    xin = din("xin", [NT, 128, D])
    cexp = din("cexp", [2, 128, D])
    sret = din("sret", [16, 4, 128, 128])
    ssre = din("ssre", [16, 2048])
    ssim = din("ssim", [16, 2048])
    w_ada = din("w_ada", [D, 6 * D])
    b_ada = din("b_ada", [1, 6 * D])
    n1g = din("norm1_g", [1, D])
    n2g = din("norm2_g", [1, D])
    nfg = din("norm_f_g", [1, D])
    w_in = din("w_in", [D, 4608])
    lam_re = din("lam_re", [32, 64])
    lam_im = din("lam_im", [32, 64])
    log_dt = din("log_dt", [1, 32])
    b_re = din("b_re", [32, 64, 16])
    b_im = din("b_im", [32, 64, 16])
    c_re = din("c_re", [512, 64])
    c_im = din("c_im", [512, 64])
    ssm_d = din("ssm_d", [1, 512])
    w_glu = din("w_glu", [512, 512])
    w_br = din("w_br", [512, D])
    w_bs = din("w_bs", [512, D])
    w_out = din("w_out", [D, D])
    w_up = din("w_up", [D, 4 * D])
    w_down = din("w_down", [4 * D, D])
    k_ident = din("k_ident", [128, 128])
    k_rot = din("k_rot", [NT, 128, 512])
    k_maskT = din("k_maskT", [2, 128, 512])
    k_qin = din("k_qin", [2, 128, 512])
    k_kout = din("k_kout", [2, 128, 4])
    k_seqf = din("k_seqf", [128, 2048])
    k_seqp = din("k_seqp", [128, 16])
    k_tmask = din("k_tmask", [128, 128])

    yout = dout("yout", [NT, 128, D])
    o_retp = dout("o_retp", [4, 128, 128])
    o_srep = dout("o_srep", [32, 64])
    o_simp = dout("o_simp", [32, 64])
    o_rets = dout("o_rets", [16, 4, 128, 128])
    o_sres = dout("o_sres", [16, 2048])
    o_sims = dout("o_sims", [16, 2048])
    x1d = nc.dram_tensor("x1d", [NT, 128, D], F32, kind="Internal").ap()
    modd = nc.dram_tensor("modd", [2, 6, 128, D], F32, kind="Internal").ap()

    es = ExitStack()
    with es:
        PS = es.enter_context(nc.psum_tensor("PS", [128, 4096], F32))
        PERS = Arena(nc, es, "pers", 2 * 1024)
        AR = Arena(nc, es, "arena", 204 * 1024)

        def bank(b):
            return PS[:, 512 * b:512 * (b + 1)]

        def bankbf(b):
            return PS[:, 512 * b:512 * (b + 1)].bitcast(BF16)

        psrr = [0]
        pslo = [0]
        pshi = [8]

        def nb():
            n = pshi[0] - pslo[0]
            b = pslo[0] + psrr[0] % n
            psrr[0] += 1
            return b

        def nb2():
            if psrr[0] % 2:
                psrr[0] += 1
            b = psrr[0] % 8
            psrr[0] += 2
            return b

        def pk(b):
            return ('ps', b)

        ukey = [0]

        def uk():
            ukey[0] += 1
            return 'u%d' % ukey[0]

        def fsz(ap):
            n = 1
            for d_ in ap.shape[1:]:
                n *= d_
            return n

        def dma_load(out, in_, key, dkey=None, eng='sp', reads=(), **kw):
            S.add(eng, lambda e, s: e.dma_start(out=out, in_=in_, **kw).then_inc(s, 16),
                  reads=list(reads), writes=[key], dma=dkey or uk(), nbytes=out.shape[0] * fsz(out) * 4)

        def dma_store(out, in_, rkeys, dkey=None, eng='sp'):
            S.add(eng, lambda e, s: e.dma_start(out=out, in_=in_).then_inc(s, 16),
                  reads=list(rkeys), writes=[], dma=dkey or uk(), nbytes=in_.shape[0] * fsz(in_) * 4)

        def op(eng, fn, r=(), w=()):
            c = getattr(fn, 'cost', None)
            if isinstance(c, tuple):
                n_ = c[1]
                c = {'dve': 0.17 + n_ / 960.0, 'act': 0.28 + n_ / 1200.0, 'pool': 0.35 + n_ / 500.0}.get(eng, 0.5)
            S.add(eng, fn, reads=list(r), writes=list(w), cost=c)

        def pe_group(fns, r, w):
            fns = list(fns)

            def run(e):
                ins = None
                for f in fns:
                    ins = f(e)
                return ins
            S.add('pe', run, reads=list(r), writes=list(w), cost=sum(getattr(f, 'cost', 0.07) for f in fns))

        def withcost(f, c):
            f.cost = c
            return f

        def MM(out, lhsT, rhs, start, stop):
            return withcost(lambda e: e.matmul(out, lhsT=lhsT, rhs=rhs, start=start, stop=stop),
                            max(0.06, 0.00042 * fsz(rhs) + 0.005) * (4.0 if rhs.dtype == F32 else 1.0))

        def TR(out, in_, idn):
            return withcost(lambda e: e.transpose(out=out, in_=in_, identity=idn), 0.065)

        def TT(out, in0, in1, o):
            return withcost(lambda e: e.tensor_tensor(out=out, in0=in0, in1=in1, op=o), ('tt', fsz(out)))

        def TS(out, in0, s1, s2, o0, o1=None):
            if o1 is None:
                return withcost(lambda e: e.tensor_scalar(out=out, in0=in0, scalar1=s1, scalar2=None, op0=o0), ('tt', fsz(out)))
            return withcost(lambda e: e.tensor_scalar(out=out, in0=in0, scalar1=s1, scalar2=s2, op0=o0, op1=o1), ('tt', fsz(out)))

        def STT(out, in0, sc, in1, o0, o1):
            return withcost(lambda e: e.scalar_tensor_tensor(out=out, in0=in0, scalar=sc, in1=in1, op0=o0, op1=o1), ('tt', fsz(out)))

        def ACT(out, in_, func, scale=None, accum_out=None):
            kw = {}
            if scale is not None:
                kw['scale'] = scale
            if accum_out is not None:
                kw['accum_out'] = accum_out
            return withcost(lambda e: e.activation(out=out, in_=in_, func=func, **kw),
                            ('tt', fsz(out) + (110 if accum_out is not None else 0)))

        def CP(out, in_):
            return withcost(lambda e: e.tensor_copy(out=out, in_=in_), ('tt', fsz(out)))

        def ACP(out, in_):
            return withcost(lambda e: e.copy(out=out, in_=in_), ('tt', fsz(out)))

        def MS(out, v):
            return withcost(lambda e: e.memset(out, v), ('tt', fsz(out)))

        def cast_load_rows(dst3, src2, nkt, key, c0=0, c1=None, dkey=None):
            dk = dkey or uk()

            def run(e, s):
                for kt in range(nkt):
                    e.dma_start(out=dst3[:, kt, :], in_=src2[kt * 128:(kt + 1) * 128, c0:c1],
                                max_dma_last_dim=4096).then_inc(s, 16)
            S.add('pool', run, reads=[], writes=[key], dma=dk, ndma=nkt, nbytes=nkt * 128 * fsz(dst3[:, 0, :]) * 4)

        MUL, ADD, SUB = ALU.mult, ALU.add, ALU.subtract

        ident = PERS.alloc([128], F32)
        identb = PERS.alloc([128], BF16)
        nhalf = PERS.alloc([4], F32)
        stat = PERS.alloc([64], F32)
        dma_load(ident, k_ident, 'ident')
        op('dve', CP(identb, ident), ['ident'], ['identb'])
        op('dve', MS(nhalf, -0.5), [], ['nhalf'])

        def type_of(T_):
            return 0 if T_ < ntp else 1

        def load_mod(tiles, ty, idxs):
            for t_, ix in zip(tiles, idxs):
                dma_load(t_, modd[ty, ix], ('modA', ix))

        def frontend(xs, xkey, A_, sh_, Akeys, shkeys, hb, hT, junk32, jk=('junk32',), hbk='hb', hTk='hT', part=0, add_eng='dve'):
            jk = list(jk)
            op('act', ACT(hb, xs, AF.Square, accum_out=stat[:, 0:1]), [xkey], [hbk, 'stat0'])
            op('dve', TS(stat[:, 1:2], stat[:, 0:1], 1.0 / D, EPS, MUL, ADD), ['stat0'], ['stat1'])
            op('pool', TT(stat[:, 2:3], stat[:, 1:2], nhalf[:, 0:1], ALU.pow), ['stat1', 'nhalf'], ['rstd'])
            if junk32 is None:
                junk32 = hb
                jk = [hbk]
            op('dve', STT(junk32, xs, stat[:, 2:3], A_, MUL, MUL), [xkey, 'rstd'] + list(Akeys), jk)
            op(add_eng, TT(hb, junk32, sh_, ADD), jk + list(shkeys), [hbk])
            if part == 1:
                return
            frontend2(hb, hT, hbk, hTk)

        def frontend2(hb, hT, hbk='hb', hTk='hT'):
            b = nb()
            pe_group([TR(bankbf(b)[:, k * 128:(k + 1) * 128], hb[:, k * 128:(k + 1) * 128], identb) for k in range(8)],
                     [hbk, 'identb'], [pk(b)])
            op('act', ACP(hT.rearrange("p a b -> p (a b)"), bankbf(b)), [pk(b)], [hTk])

        def phase0():
            AR.reset()
            cx = AR.alloc([2, D], F32)
            csg = AR.alloc([2, D], F32)
            cb = AR.alloc([2, D], BF16)
            cT = AR.alloc([2, 8, 128], BF16)
            gbc = AR.alloc([2, D], F32)
            wada = [AR.alloc([8, 512], BF16) for _ in range(2)]
            bbc = [AR.alloc([512], F32) for _ in range(2)]
            mo = [AR.alloc([512], F32) for _ in range(4)]
            mtmp = AR.alloc([512], F32)
            for ty in range(2):
                dma_load(cx[:, ty, :], cexp[ty], ('cx', ty))
            dma_load(gbc[:, 0, :], n1g.partition_broadcast(128), ('gbc', 0))
            dma_load(gbc[:, 1, :], n2g.partition_broadcast(128), ('gbc', 1))
            for ty in range(2):
                op('act', ACT(csg[:, ty, :], cx[:, ty, :], AF.Sigmoid), [('cx', ty)], [('csg', ty)])
                op('dve', TT(cb[:, ty, :], cx[:, ty, :], csg[:, ty, :], MUL), [('cx', ty), ('csg', ty)], [('cb', ty)])
                b = nb()
                pe_group([TR(bankbf(b)[:, k * 128:(k + 1) * 128], cb[:, ty, k * 128:(k + 1) * 128], identb) for k in range(8)],
                         [('cb', ty), 'identb'], [pk(b)])
                op('act', ACP(cT[:, ty, :, :].rearrange("p a b -> p (a b)"), bankbf(b)), [pk(b)], [('cT', ty)])
            for blk in range(12):
                sl = blk % 2
                which = blk // 2
                cast_load_rows(wada[sl], w_ada, 8, ('wada', sl), blk * 512, (blk + 1) * 512, dkey='wada%d' % sl)
                dma_load(bbc[sl], b_ada[:, blk * 512:(blk + 1) * 512].partition_broadcast(128), ('bbc', sl), 'bbc%d' % sl)
                for ty in range(2):
                    b = nb()
                    pe_group([MM(bank(b), cT[:, ty, k, :], wada[sl][:, k, :], k == 0, k == 7) for k in range(8)],
                             [('cT', ty), ('wada', sl)], [pk(b)])
                    half = blk % 2
                    cs = slice(half * 512, (half + 1) * 512)
                    mi_ = (blk * 2 + ty) % 4
                    dest = mo[mi_]
                    dkey = ('mo', mi_)
                    if which in (1, 4):
                        gi = 0 if which == 1 else 1
                        op('dve', STT(mtmp, bank(b), 1.0, bbc[sl], ADD, ADD), [pk(b), ('bbc', sl)], ['mtmp'])
                        op('dve', TT(dest, mtmp, gbc[:, gi, cs], MUL), ['mtmp', ('gbc', gi)], [dkey])
                    else:
                        op('dve', TT(dest, bank(b), bbc[sl], ADD), [pk(b), ('bbc', sl)], [dkey])
                    dma_store(modd[ty, which][:, cs], dest, [dkey], 'mo%d' % mi_)
            assert AR.off <= PH0_BYTES, AR.off

        crT_box = {}
        PH0_BYTES = 64 * 1024

        def passR():
            AR.reset()
            crT = AR.alloc([8, NT * 128], BF16)
            crT_box['crT'] = crT
            crT_box['keep'] = AR.off
            winR = AR.alloc([8, 3072], BF16)
            modR = [AR.alloc([D], F32) for _ in range(2)]
            wbr = AR.alloc([4, D], BF16)
            xs2 = [AR.alloc([D], F32) for _ in range(2)]
            rot2 = [AR.alloc([512], F32) for _ in range(2)]
            junk32 = AR.alloc([D], F32)
            hb = AR.alloc([D], BF16)
            hT = AR.alloc([8, 128], BF16)
            maskT = AR.alloc([2, 4, 128], F32)
            qinb = AR.alloc([2, 4, 128], BF16)
            koutp = AR.alloc([2, 4], F32)
            seqf = AR.alloc([16, 128], BF16)
            seqp = AR.alloc([16], F32)
            rt = [AR.alloc([4, 128], F32) for _ in range(2)]
            qr = AR.alloc([4, 128], BF16)
            kr = AR.alloc([4, 128], BF16)
            kd = AR.alloc([4, 128], BF16)
            vb = AR.alloc([512], BF16)
            sg = AR.alloc([512], F32)
            gs = AR.alloc([4, 128], F32)
            qkT = AR.alloc([8, 128], BF16)
            qsT = AR.alloc([4, 128], BF16)
            Pm = AR.alloc([4, 128], BF16)
            retf = AR.alloc([4, 128], BF16)
            retT = AR.alloc([4, 128], BF16)
            Rf = AR.alloc([4, 128], F32)
            Rb = AR.alloc([4, 128], BF16)
            sgr = AR.alloc([8, 128], BF16)
            R0f = [AR.alloc([4, 128], F32) for _ in range(2)]
            R0b = [AR.alloc([4, 128], BF16) for _ in range(2)]
            qsm = [AR.alloc([4, 128], BF16) for _ in range(2)]
            kdm = [AR.alloc([4, 128], BF16) for _ in range(2)]
            Rn = [AR.alloc([4, 128], F32) for _ in range(2)]

            cast_load_rows(winR[:, :, 0:2048], w_in, 8, 'winR0', 0, 2048)
            cast_load_rows(winR[:, :, 2048:3072], w_in, 8, 'winR1', 2560, 3584)
            cast_load_rows(wbr, w_br, 4, 'wbr')
            mT2 = maskT.rearrange("p t h i -> p t (h i)")
            dma_load(mT2[:, 0, :], k_maskT[0], ('maskT', 0))
            dma_load(mT2[:, 1, :], k_maskT[1], ('maskT', 1))
            for ty in range(2):
                dma_load(qinb[:, ty].rearrange("p h i -> p (h i)"), k_qin[ty], ('qinb', ty), eng='pool')
                dma_load(koutp[:, ty, :], k_kout[ty], ('koutp', ty))
            dma_load(seqf.rearrange("p b i -> p (b i)"), k_seqf, 'seqf', eng='pool', max_dma_last_dim=4096)
            dma_load(seqp, k_seqp, 'seqp')
            op('dve', MS(Rf, 0.0), [], ['Rf'])
            op('dve', MS(Rb, 0.0), [], ['Rb'])

            def load_x(T_):
                sl = T_ % 2
                dma_load(xs2[sl], xin[T_], ('xs', sl), 'xR%d' % sl)
                dma_load(rot2[sl], k_rot[T_], ('rot', sl), 'rot%d' % sl)

            winS_pf = AR.alloc_tail([8, 1536], BF16)
            wglu_pf = AR.alloc_tail([4, 512], BF16)
            cast_load_rows(winS_pf[:, :, 0:512], w_in, 8, 'winS0', 2048, 2560)
            cast_load_rows(winS_pf[:, :, 512:1536], w_in, 8, 'winS1', 3584, 4608)
            cast_load_rows(wglu_pf, w_glu, 4, 'wglu')
            load_x(0)
            load_mod(modR, 0, [0, 1])
            pbanks = {}
            sgr2 = [sgr, AR.alloc([8, 128], BF16)]
            hbj = AR.alloc([512], BF16)

            def stA(T_):
                ty = type_of(T_)
                sl = T_ % 2
                if T_ + 1 < NT:
                    load_x(T_ + 1)
                if T_ == ntp:
                    load_mod(modR, 1, [0, 1])
                frontend(xs2[sl], ('xs', sl), modR[1], modR[0], [('modA', 1)], [('modA', 0)], hb, hT, junk32)
                bq, bk, bv, bg = 0, 1, 2, 3
                for bi, bb in enumerate((bq, bk, bv, bg)):
                    pe_group([MM(bank(bb), hT[:, k, :], winR[:, k, bi * 512:(bi + 1) * 512], k == 0, k == 7) for k in range(8)],
                             ['hT', 'winR0'], [pk(bb)])
                pbanks[T_] = (bq, bk, bv, bg)
                for hh in range(2):
                    bb = nb()
                    fns = []
                    for mi in range(4):
                        m = hh * 4 + mi
                        for k in range(8):
                            fns.append(MM(bank(bb)[:, mi * 128:(mi + 1) * 128], winR[:, k, 2048 + m * 128:2048 + (m + 1) * 128], hT[:, k, :], k == 0, k == 7))
                    pe_group(fns, ['hT', 'winR1'], [pk(bb)])
                    op('act', ACT(sgr2[sl][:, hh * 4:(hh + 1) * 4, :].rearrange("p a b -> p (a b)"), bank(bb), AF.Sigmoid), [pk(bb)], [('sgr', sl, hh)])

            def stB1(T_):
                ty = type_of(T_)
                sl = T_ % 2
                rot = rot2[sl]
                rk = ('rot', sl)
                bq, bk, bv, bg = pbanks[T_]
                for (bb, off, dst, dkey) in ((bq, 0, qr, 'qr'), (bk, 256, kr, 'kr')):
                    src3 = bank(bb).rearrange("p (h d) -> p h d", h=4)
                    c2 = rot[:, off:off + 128].unsqueeze(1).to_broadcast([128, 4, 128])
                    sn = rot[:, off + 128:off + 192].unsqueeze(1).to_broadcast([128, 4, 64])
                    sp_ = rot[:, off + 192:off + 256].unsqueeze(1).to_broadcast([128, 4, 64])
                    op('dve', TT(rt[0], src3, c2, MUL), [pk(bb), rk], ['rt0'])
                    op('dve', TT(rt[1][:, :, 0:64], src3[:, :, 64:128], sn, MUL), [pk(bb), rk], ['rt1'])
                    op('dve', TT(rt[1][:, :, 64:128], src3[:, :, 0:64], sp_, MUL), [pk(bb), rk], ['rt1'])
                    op('pool', TT(dst, rt[0], rt[1], ADD), ['rt0', 'rt1'], [dkey])
                op('pool', TT(kd, kr, koutp[:, ty, :].unsqueeze(2).to_broadcast([128, 4, 128]), MUL), ['kr', ('koutp', ty)], ['kd'])
                op('act', ACP(vb, bank(bv)), [pk(bv)], ['vb'])
                op('act', ACT(sg, bank(bg), AF.Sigmoid), [pk(bg)], ['sg'])
                op('dve', TT(gs.rearrange("p h d -> p (h d)"), bank(bg), sg, MUL), [pk(bg), 'sg'], ['gs'])

            def stB2(T_):
                ty = type_of(T_)
                sl = T_ % 2
                bt = nb()
                pe_group([TR(bankbf(bt)[:, h * 128:(h + 1) * 128], qr[:, h, :], identb) for h in range(4)] +
                         [TR(bankbf(bt)[:, (4 + h) * 128:(5 + h) * 128], kr[:, h, :], identb) for h in range(4)],
                         ['qr', 'kr', 'identb'], [pk(bt)])
                op('act', ACP(qkT.rearrange("p a b -> p (a b)"), bankbf(bt)), [pk(bt)], ['qkT'])
                op('pool', TT(qsT, qkT[:, 0:4, :], qinb[:, ty], MUL), ['qkT', ('qinb', ty)], ['qsT'])
                bs_ = nb()
                pe_group([MM(bank(bs_)[:, h * 128:(h + 1) * 128], qkT[:, 4 + h, :], qkT[:, h, :], True, True) for h in range(4)],
                         ['qkT'], [pk(bs_)])
                op('dve', TT(Pm, bank(bs_).rearrange("p (h i) -> p h i", h=4), maskT[:, ty], MUL), [pk(bs_), ('maskT', ty)], ['Pm'])
                if ty == 0:
                    bo = nb()
                    fns = []
                    for h in range(4):
                        fns.append(MM(bank(bo)[:, h * 128:(h + 1) * 128], Pm[:, h, :], vb[:, h * 128:(h + 1) * 128], True, False))
                        fns.append(MM(bank(bo)[:, h * 128:(h + 1) * 128], qsT[:, h, :], Rb[:, h, :], False, True))
                    pe_group(fns, ['Pm', 'vb', 'qsT', 'Rb'], [pk(bo)])
                    osl = [bank(bo)[:, h * 128:(h + 1) * 128] for h in range(4)]
                    okeys = [pk(bo)]
                else:
                    bos = [0, 1, 2, 3]
                    pe_group([MM(bank(bos[h])[:, 0:128], Pm[:, h, :], vb[:, h * 128:(h + 1) * 128], True, False) for h in range(4)],
                             ['Pm', 'vb'], [pk(x) for x in bos])
                    for b_ in range(16):
                        s2 = b_ % 2
                        dma_load(R0f[s2], sret[b_].rearrange("h d e -> d h e"), ('R0f', s2), 'R0fa%d' % s2)
                        op('act', ACP(R0b[s2], R0f[s2]), [('R0f', s2)], [('R0b', s2)])
                        op('dve' if b_ % 2 else 'pool', TT(qsm[s2], qsT, seqf[:, b_, :].unsqueeze(1).to_broadcast([128, 4, 128]), MUL), ['qsT', 'seqf'], [('qsm', s2)])
                        pe_group([MM(bank(bos[h])[:, 0:128], qsm[s2][:, h, :], R0b[s2][:, h, :], False, b_ == 15) for h in range(4)],
                                 [('qsm', s2), ('R0b', s2)], [pk(x) for x in bos])
                        op('act', ACT(kdm[s2], kd, AF.Copy, scale=seqp[:, b_:b_ + 1]), ['kd', 'seqp'], [('kdm', s2)])
                        bkv = nb()
                        pe_group([MM(bank(bkv)[:, h * 128:(h + 1) * 128], kdm[s2][:, h, :], vb[:, h * 128:(h + 1) * 128], True, True) for h in range(4)],
                                 [('kdm', s2), 'vb'], [pk(bkv)])
                        for h in range(4):
                            op('dve', STT(Rn[s2][:, h, :], R0f[s2][:, h, :], float(cd[1][h]), bank(bkv)[:, h * 128:(h + 1) * 128], MUL, ADD),
                               [pk(bkv), ('R0f', s2)], [('Rn', s2)])
                        dma_store(o_rets[b_].rearrange("h d e -> d h e"), Rn[s2], [('Rn', s2)], 'o_rets%d' % s2)
                    osl = [bank(bos[h])[:, 0:128] for h in range(4)]
                    okeys = [pk(x) for x in bos]
                for h in range(4):
                    op('act', ACT(hbj[:, h * 128:(h + 1) * 128], osl[h], AF.Square, accum_out=stat[:, 8 + h:9 + h]), okeys, ['hbj', ('ss4', h)])
                op('dve', TS(stat[:, 12:16], stat[:, 8:12], 1.0 / 128, EPS, MUL, ADD), [('ss4', h) for h in range(4)], ['ms4'])
                op('pool', TT(stat[:, 16:20], stat[:, 12:16], nhalf, ALU.pow), ['ms4', 'nhalf'], ['rstd4'])
                for h in range(4):
                    op('dve', STT(retf[:, h, :], osl[h], stat[:, 16 + h:17 + h], gs[:, h, :], MUL, MUL), okeys + ['rstd4', 'gs'], ['retf'])
                brt = nb()
                pe_group([TR(bankbf(brt)[:, h * 128:(h + 1) * 128], retf[:, h, :], identb) for h in range(4)], ['retf', 'identb'], [pk(brt)])
                op('act', ACP(retT.rearrange("p a b -> p (a b)"), bankbf(brt)[:, 0:512]), [pk(brt)], ['retT'])
                if ty == 0:
                    bkv = nb()
                    pe_group([MM(bank(bkv)[:, h * 128:(h + 1) * 128], kd[:, h, :], vb[:, h * 128:(h + 1) * 128], True, True) for h in range(4)],
                             ['kd', 'vb'], [pk(bkv)])
                    for h in range(4):
                        op('dve', STT(Rf[:, h, :], Rf[:, h, :], float(cd[0][h]), bank(bkv)[:, h * 128:(h + 1) * 128], MUL, ADD), [pk(bkv)], ['Rf'])
                    op('pool', CP(Rb, Rf), ['Rf'], ['Rb'])
                    if T_ == ntp - 1:
                        dma_store(o_retp.rearrange("h d e -> d h e"), Rf, ['Rf'])
                else:
                    pass
                for hh in range(2):
                    bb = nb()
                    fns = []
                    for mi in range(4):
                        m = hh * 4 + mi
                        for kc in range(4):
                            fns.append(MM(bank(bb)[:, mi * 128:(mi + 1) * 128], wbr[:, kc, m * 128:(m + 1) * 128], retT[:, kc, :], kc == 0, kc == 3))
                    pe_group(fns, ['retT', 'wbr'], [pk(bb)])
                    op('dve', TT(crT[:, hh * 4:(hh + 1) * 4, T_ * 128:(T_ + 1) * 128], bank(bb).rearrange("p (a b) -> p a b", a=4),
                                 sgr2[sl][:, hh * 4:(hh + 1) * 4, :], MUL), [pk(bb), ('sgr', sl, hh)], [('crT', T_, hh)])

            pslo[0] = 4
            stA(0)
            for T_ in range(NT):
                stB1(T_)
                if T_ + 1 < NT:
                    stA(T_ + 1)
                stB2(T_)
            pslo[0] = 0

        s5 = {}
        S5KEYS = ['WstRe', 'WstIm', 'Toep', 'CL', 'L4', 'dbc', 'tabs']

        def alloc_s5():
            o0 = AR.off
            WstRe = AR.alloc([32, 64], BF16)
            WstIm = AR.alloc([32, 64], BF16)
            Toep = AR.alloc([16, 128], BF16)
            CLre = AR.alloc([32, 128], BF16)
            CLim = AR.alloc([32, 128], BF16)
            L4 = AR.alloc([4, 32], F32)
            dbc = AR.alloc([512], F32)
            cosT = AR.alloc([32, 32], F32)
            sinT = AR.alloc([32, 32], F32)
            rpow = AR.alloc([32, 32], F32)
            s5.update(WstRe=WstRe, WstIm=WstIm, Toep=Toep, CLre=CLre, CLim=CLim, L4=L4, dbc=dbc,
                      cosT=cosT, sinT=sinT, rpow=rpow, keep=AR.off,
                      region=AR.t[:, o0 // 4:AR.off // 4], nwords=(AR.off - o0) // 4)
            sizes = [('WstRe', 1024), ('WstIm', 1024), ('Toep', 1024), ('CLre', 2048), ('CLim', 2048), ('L4', 128),
                     ('dbc', 512), ('cosT', 1024), ('sinT', 1024), ('rpow', 1024)]
            o_ = 0
            s5['woff'] = {}
            for nm, nw in sizes:
                s5['woff'][nm] = (o_, nw)
                o_ += nw
            assert o_ == s5['nwords'], (o_, s5['nwords'])

        def s5setup():
            AR.reset(PH0_BYTES)
            alloc_s5()
            op('dve', MS(s5['region'], 0.0), [], S5KEYS)
            WstRe, WstIm, Toep, CLre, CLim, L4, dbc, cosT, sinT, rpow = (s5[k] for k in
                ('WstRe', 'WstIm', 'Toep', 'CLre', 'CLim', 'L4', 'dbc', 'cosT', 'sinT', 'rpow'))
            tC = [AR.alloc([32, 16], F32) for _ in range(4)]
            tW = AR.alloc([4, 32], F32)
            lamT = AR.alloc([256], F32)
            lamp = AR.alloc([2, 32], F32)
            dtb = AR.alloc([32], F32)
            Bre = AR.alloc([32, 16], F32)
            Bim = AR.alloc([32, 16], F32)
            Cnat = AR.alloc([4, 2, 64], F32)
            Cre = AR.alloc([32, 16], F32)
            Cim = AR.alloc([32, 16], F32)
            tA = [AR.alloc([32], F32) for _ in range(12)]
            tI = AR.alloc([32], I32)
            PW = AR.alloc([8, 2, 32], F32)
            EL = AR.alloc([4, 2, 32], F32)
            WallRe = AR.alloc([32, 128], F32)
            WallIm = AR.alloc([32, 128], F32)
            RmRe = AR.alloc([32, 128], F32)
            RmIm = AR.alloc([32, 128], F32)
            tB = [AR.alloc([16, 16], F32) for _ in range(4)]
            tmask = AR.alloc([128], F32)
            P64 = slice(0, 64)

            def ld_lam(e, s):
                e.dma_start(out=lamT[0:32, 0:64], in_=lam_re).then_inc(s, 16)
                e.dma_start(out=lamT[0:32, 64:128], in_=lam_im).then_inc(s, 16)
            S.add('sp', ld_lam, reads=[], writes=['lamT'], dma=uk(), ndma=2)
            dma_load(dtb[P64, :], log_dt.partition_broadcast(64), 'dtb')
            def ld_b(e, s_):
                for g4 in range(8):
                    e.dma_start(out=Bre[P64, g4 * 4:(g4 + 1) * 4, :], in_=b_re[g4 * 4:(g4 + 1) * 4].rearrange("g p m -> p g m")).then_inc(s_, 16)
                    e.dma_start(out=Bim[P64, g4 * 4:(g4 + 1) * 4, :], in_=b_im[g4 * 4:(g4 + 1) * 4].rearrange("g p m -> p g m")).then_inc(s_, 16)
            S.add('sp', ld_b, reads=[], writes=['Bre', 'Bim'], dma=uk(), ndma=16)

            def ld_c(e, s):
                for ti in range(4):
                    e.dma_start(out=Cnat[:, ti, 0, :], in_=c_re[ti * 128:(ti + 1) * 128, :]).then_inc(s, 16)
                    e.dma_start(out=Cnat[:, ti, 1, :], in_=c_im[ti * 128:(ti + 1) * 128, :]).then_inc(s, 16)
            S.add('sp', ld_c, reads=[], writes=['Cnat'], dma=uk(), ndma=8)
            dma_load(dbc, ssm_d.partition_broadcast(128), 'dbc')
            dma_load(tmask, k_tmask, 'tmask')

            b = nb()
            pe_group([TR(bank(b)[0:64, 0:32], lamT[0:32, 0:64], ident[0:32, 0:32]),
                      TR(bank(b)[0:64, 32:64], lamT[0:32, 64:128], ident[0:32, 0:32])],
                     ['lamT', 'ident'], [pk(b)])
            op('act', ACP(lamp[P64].rearrange("p a b -> p (a b)"), bank(b)[0:64, 0:64]), [pk(b)], ['lamp'])
            for part, Cdst, ckey in ((0, Cre, 'Cre'), (1, Cim, 'Cim')):
                b = nb()
                pe_group([TR(bank(b)[0:64, ti * 128:(ti + 1) * 128], Cnat[:, ti, part, :], ident) for ti in range(4)],
                         ['Cnat', 'ident'], [pk(b)])
                op('act', ACP(Cdst[P64].rearrange("p g m -> p (g m)"), bank(b)[0:64, :]), [pk(b)], [ckey])

            off_save = AR.off
            yield
            AR.off = off_save
            K = ['s5t']
            lre = lamp[P64, 0, :]
            lim = lamp[P64, 1, :]
            T = [t[P64] for t in tA]

            def dv(fn, r=(), w=()):
                op('dve', fn, list(r) + K, list(w) + K)

            def ac(fn, r=(), w=()):
                op('act', fn, list(r) + K, list(w) + K)

            def pw(l, part):
                return PW[P64, l + 3, part, :]

            ac(ACT(T[0], dtb[P64], AF.Exp), ['dtb', 'lamp'])
            dv(TT(T[1], lre, T[0], MUL))
            dv(TT(T[2], lim, T[0], MUL))
            ac(ACT(T[3], T[1], AF.Exp))
            ac(ACT(T[4], T[1], AF.Exp, scale=-1.0))
            ac(ACT(tW[P64, 2, :], T[1], AF.Exp, scale=4.0))
            ac(ACT(tW[P64, 3, :], T[1], AF.Exp, scale=-4.0))

            def sin_of(dst, src_ang, shift):
                dv(TS(T[5], src_ang, shift, None, ADD))
                dv(TS(tI[P64], T[5], 1.0 / TWO_PI, None, MUL))
                dv(CP(T[6], tI[P64]))
                dv(STT(T[7], T[6], -TWO_PI, T[5], MUL, ADD))
                dv(TS(T[7], T[7], 3.1415925, -3.1415925, ALU.min, ALU.max))
                ac(ACT(dst, T[7], AF.Sin))

            sin_of(T[8], T[2], 0.0)
            sin_of(T[9], T[2], math.pi / 2.0)
            dv(TT(pw(1, 0), T[3], T[9], MUL))
            dv(TT(pw(1, 1), T[3], T[8], MUL))
            dv(TT(pw(-1, 0), T[4], T[9], MUL))
            dv(STT(pw(-1, 1), T[4], -1.0, T[8], MUL, MUL))
            dv(MS(pw(0, 0), 1.0))
            dv(MS(pw(0, 1), 0.0))

            def cmul(ore, oim, are, aim, bre, bim):
                dv(TT(T[5], are, bre, MUL))
                dv(TT(T[6], aim, bim, MUL))
                dv(TT(T[7], are, bim, MUL))
                dv(TT(T[10], aim, bre, MUL))
                dv(TT(ore, T[5], T[6], SUB))
                dv(TT(oim, T[7], T[10], ADD))

            for l in (2, 3, 4):
                cmul(pw(l, 0), pw(l, 1), pw(l - 1, 0), pw(l - 1, 1), pw(1, 0), pw(1, 1))
            for l in (-2, -3):
                cmul(pw(l, 0), pw(l, 1), pw(l + 1, 0), pw(l + 1, 1), pw(-1, 0), pw(-1, 1))
            K1 = list(K)

            def dv(fn, r=(), w=()):
                op('dve', fn, list(r) + ['s5t', 's5k'], list(w) + ['s5k'])

            def cmul(ore, oim, are, aim, bre, bim):
                dv(TT(T[5], are, bre, MUL))
                dv(TT(T[6], aim, bim, MUL))
                dv(TT(T[7], are, bim, MUL))
                dv(TT(T[10], aim, bre, MUL))
                dv(TT(ore, T[5], T[6], SUB))
                dv(TT(oim, T[7], T[10], ADD))
            dv(TS(T[0], pw(1, 0), -1.0, None, ADD))
            dv(TT(T[1], lre, lre, MUL))
            dv(TT(T[2], lim, lim, MUL))
            dv(TT(T[1], T[1], T[2], ADD))
            dv(lambda e: e.reciprocal(out=T[1], in_=T[1]))
            dv(TT(T[2], T[0], lre, MUL))
            dv(TT(T[3], pw(1, 1), lim, MUL))
            dv(TT(T[2], T[2], T[3], ADD))
            dv(TT(T[8], T[2], T[1], MUL))
            dv(TT(T[2], pw(1, 1), lre, MUL))
            dv(TT(T[3], T[0], lim, MUL))
            dv(TT(T[2], T[2], T[3], SUB))
            dv(TT(T[9], T[2], T[1], MUL))
            for l in range(4):
                cmul(EL[P64, l, 0, :], EL[P64, l, 1, :], pw(l, 0), pw(l, 1), T[8], T[9])
            dv(CP(L4[P64, 0, :], pw(4, 0)), w=['L4'])
            dv(CP(L4[P64, 1, :], pw(4, 1)), w=['L4'])
            dv(TS(L4[P64, 2, :], pw(4, 1), -1.0, None, MUL), w=['L4'])

            tP = [AR.alloc([32], F32)[P64] for _ in range(4)]

            def dv(fn, r=(), w=()):
                op('pool', fn, list(r) + ['s5t', 's5p'], list(w) + ['s5p'])

            def cmul(ore, oim, are, aim, bre, bim):
                dv(TT(tP[0], are, bre, MUL))
                dv(TT(tP[1], aim, bim, MUL))
                dv(TT(tP[2], are, bim, MUL))
                dv(TT(tP[3], aim, bre, MUL))
                dv(TT(ore, tP[0], tP[1], SUB))
                dv(TT(oim, tP[2], tP[3], ADD))
            c64 = cosT[P64]
            s64 = sinT[P64]
            dv(MS(rpow[P64, :, 0:1], 0.0), w=['tabs'])
            dv(CP(rpow[P64, :, 1:32], tW[P64, 2, :].unsqueeze(2).to_broadcast([64, 32, 31])), w=['tabs'])
            dv(MS(c64[:, :, 0:1], 1.0), w=['tabs'])
            dv(MS(s64[:, :, 0:1], 0.0), w=['tabs'])
            dv(TT(c64[:, :, 1], pw(4, 0), tW[P64, 3, :], MUL), w=['tabs'])
            dv(TT(s64[:, :, 1], pw(4, 1), tW[P64, 3, :], MUL), w=['tabs'])
            dv(CP(tW[P64, 0, :], c64[:, :, 1]))
            dv(CP(tW[P64, 1, :], s64[:, :, 1]))
            tc = [t[P64] for t in tC]
            for k in (2, 4, 8, 16):
                cmul(tW[P64, 0, :], tW[P64, 1, :], c64[:, :, k // 2], s64[:, :, k // 2], c64[:, :, k // 2], s64[:, :, k // 2])
                wr = tW[P64, 0, :].unsqueeze(2).to_broadcast([64, 32, k])
                wi = tW[P64, 1, :].unsqueeze(2).to_broadcast([64, 32, k])
                a_ = [t[:, :, 0:k] for t in tc]
                dv(TT(a_[0], c64[:, :, 0:k], wr, MUL))
                dv(TT(a_[1], s64[:, :, 0:k], wi, MUL))
                dv(TT(a_[2], c64[:, :, 0:k], wi, MUL))
                dv(TT(a_[3], s64[:, :, 0:k], wr, MUL))
                dv(TT(c64[:, :, k:2 * k], a_[0], a_[1], SUB), w=['tabs'])
                dv(TT(s64[:, :, k:2 * k], a_[2], a_[3], ADD), w=['tabs'])
            def dv(fn, r=(), w=()):
                op('dve', fn, list(r) + ['s5m'], list(w) + ['s5m'])
            dv(MS(WallRe[P64], 0.0), w=['Wall'])
            dv(MS(WallIm[P64], 0.0), w=['Wall'])
            dv(MS(RmRe[P64], 0.0), w=['Rm'])
            dv(MS(RmIm[P64], 0.0), w=['Rm'])
            dv(MS(CLre[P64], 0.0), w=['CL'])
            dv(MS(CLim[P64], 0.0), w=['CL'])

            def gsel(ap3, x):
                return ap3.rearrange("p (gp x) m -> p gp x m", x=2)[:, :, x, :]

            def bsel(ap2, x):
                return ap2.rearrange("p (gp x) -> p gp x", x=2)[:, :, x].unsqueeze(2).to_broadcast([64, 16, 16])

            def dsel(W, x, j):
                return W.rearrange("p (gp x) c -> p gp x c", x=2)[:, :, x, j * 32 + x * 16: j * 32 + x * 16 + 16]

            NLANE = 2
            lanes = [[t[P64] for t in tB]] + [[AR.alloc([16, 16], F32)[P64] for _ in range(4)] for _ in range(NLANE - 1)]
            lane_ctr = [0]
            wall_keys, rm_keys, cl_keys = [], [], []

            def cprod(ar, ai, br, bi, dre, dim_, rkeys, okey, neg_im):
                ln = lane_ctr[0] % NLANE
                lane_ctr[0] += 1
                t_ = lanes[ln]
                tk = [('tbk', ln, i_) for i_ in range(4)]
                rk_ = list(rkeys)
                op('dve', TT(t_[0], ar, br, MUL), rk_, [tk[0]])
                op('dve', TT(t_[1], ai, bi, MUL), rk_, [tk[1]])
                op('dve', TT(t_[2], ai if neg_im else ar, br if neg_im else bi, MUL), rk_, [tk[2]])
                op('dve', TT(t_[3], ar if neg_im else ai, bi if neg_im else br, MUL), rk_, [tk[3]])
                op('dve', TT(dre, t_[0], t_[1], SUB), [tk[0], tk[1], okey[0]], [okey[1]])
                if neg_im:
                    op('dve', STT(dim_, t_[2], -1.0, t_[3], MUL, SUB), [tk[2], tk[3], okey[0]], [okey[2]])
                else:
                    op('dve', TT(dim_, t_[2], t_[3], ADD), [tk[2], tk[3], okey[0]], [okey[2]])

            for l in range(4):
                j = 3 - l
                for x in range(2):
                    k1, k2 = ('Wall', 0, x, j), ('Wall', 1, x, j)
                    wall_keys += [k1, k2]
                    cprod(bsel(EL[P64, l, 0, :], x), bsel(EL[P64, l, 1, :], x), gsel(Bre[P64], x), gsel(Bim[P64], x),
                          dsel(WallRe[P64], x, j), dsel(WallIm[P64], x, j), ['Bre', 'Bim', 's5k'], ('Wall', k1, k2), False)
            for s_ in range(4):
                for (pwr, Rr, Ri, wk, klist) in ((s_ - 3, RmRe, RmIm, 'Rm', rm_keys), (s_ + 1, CLre, CLim, 'CL', cl_keys)):
                    for x in range(2):
                        k1, k2 = (wk, 0, x, s_), (wk, 1, x, s_)
                        klist += [k1, k2]
                        cprod(bsel(pw(pwr, 0), x), bsel(pw(pwr, 1), x), gsel(Cre[P64], x), gsel(Cim[P64], x),
                              dsel(Rr[P64], x, s_), dsel(Ri[P64], x, s_), ['Cre', 'Cim', 's5t'], (wk, k1, k2), True)
            for part, Wl, Wd, wkey in ((0, WallRe, WstRe, 'WstRe'), (1, WallIm, WstIm, 'WstIm')):
                for g8 in range(4):
                    b = nb()
                    pe_group([TR(bank(b)[:, gi * 64:(gi + 1) * 64], Wl[P64, g8 * 8 + gi, :], ident[0:64, 0:64]) for gi in range(8)],
                             ['Wall', 'ident'] + K + wall_keys, [pk(b)])
                    op('act', ACP(Wd[:, g8 * 8:(g8 + 1) * 8, :].rearrange("p g q -> p (g q)"), bank(b)), [pk(b)], [wkey])
            for gq in range(4):
                b = nb()
                fns = []
                for gi in range(4):
                    gp = gq * 4 + gi
                    seq = [(WallRe, RmRe, 2 * gp), (WallIm, RmIm, 2 * gp), (WallRe, RmRe, 2 * gp + 1), (WallIm, RmIm, 2 * gp + 1)]
                    for si, (Wl, Rm, g) in enumerate(seq):
                        fns.append(MM(bank(b)[:, gi * 128:(gi + 1) * 128], Wl[P64, g, :], Rm[P64, g, :], si == 0, si == 3))
                pe_group(fns, ['Wall', 'Rm'] + K + wall_keys + rm_keys, [pk(b)])
                op('dve', TT(Toep[:, gq * 4:(gq + 1) * 4, :], bank(b).rearrange("p (a c) -> p a c", a=4),
                             tmask.unsqueeze(1).to_broadcast([128, 4, 128]), MUL), [pk(b), 'tmask'], ['Toep'])
            s5['dram'] = nc.dram_tensor("s5d", [128, s5['nwords']], F32, kind="Internal").ap()
            dma_store(s5['dram'], s5['region'], S5KEYS + cl_keys)

        def passS():
            AR.reset(crT_box['keep'])
            crT = crT_box['crT']
            sd = s5['dram']
            wo = s5['woff']
            WstRe = AR.alloc([32, 128], BF16)
            WstIm = AR.alloc([32, 128], BF16)
            Toep = AR.alloc([16, 128], BF16)
            CLre = AR.alloc([32, 128], BF16)
            CLim = AR.alloc([32, 128], BF16)
            L4 = AR.alloc([4, 16], F32)
            dbc = AR.alloc([512], F32)
            cosT = AR.alloc([16, 32], F32)
            sinT = AR.alloc([16, 32], F32)
            rpow = AR.alloc([16, 32], F32)
            for t_ in (WstRe, WstIm, CLre, CLim):
                op('pool', MS(t_, 0.0), [], ['s5z'])

            def wv(t_):
                a_, b_ = t_.shape[1], t_.shape[2]
                return t_.rearrange("p a b -> p (a b)").bitcast(F32).rearrange("p (a b) -> p a b", a=a_)

            def park(nm, rows=slice(0, 128)):
                o_, n_ = wo[nm]
                return sd[rows, o_:o_ + n_]

            def ld_s5(e, s_):
                n = 0
                for Wt, nm in ((WstRe, 'WstRe'), (WstIm, 'WstIm')):
                    src = park(nm).rearrange("p (g q) -> p g q", g=32)
                    for gh in range(2):
                        e.dma_start(out=wv(Wt)[:, gh * 16:(gh + 1) * 16, gh * 32:(gh + 1) * 32],
                                    in_=src[:, gh * 16:(gh + 1) * 16, :]).then_inc(s_, 16); n += 1
                e.dma_start(out=wv(Toep).rearrange("p a b -> p (a b)"), in_=park('Toep')).then_inc(s_, 16); n += 1
                for Ct, nm in ((CLre, 'CLre'), (CLim, 'CLim')):
                    src = park(nm, slice(0, 64)).rearrange("p (g q) -> p g q", g=32)
                    for gh in range(2):
                        e.dma_start(out=wv(Ct)[64 * gh:64 * gh + 64, gh * 16:(gh + 1) * 16, :],
                                    in_=src[:, gh * 16:(gh + 1) * 16, :]).then_inc(s_, 16); n += 1
                srcL = park('L4', slice(0, 64)).rearrange("p (c g) -> p c g", c=4)
                for gh in range(2):
                    e.dma_start(out=L4[64 * gh:64 * gh + 64], in_=srcL[:, :, gh * 16:(gh + 1) * 16]).then_inc(s_, 16); n += 1
                e.dma_start(out=dbc, in_=park('dbc')).then_inc(s_, 16); n += 1
                for Tt, nm in ((cosT, 'cosT'), (sinT, 'sinT'), (rpow, 'rpow')):
                    src = park(nm, slice(0, 64)).rearrange("p (g n) -> p g n", g=32)
                    for gh in range(2):
                        e.dma_start(out=Tt[64 * gh:64 * gh + 64], in_=src[:, gh * 16:(gh + 1) * 16, :]).then_inc(s_, 16); n += 1
                assert n == NLD, n
            NLD = 4 + 1 + 4 + 2 + 1 + 6
            S.add('sp', ld_s5, reads=['s5z'], writes=list(S5KEYS), dma=uk(), ndma=NLD, nbytes=6 * 1024 * 1024)

            winS = AR.alloc_tail([8, 1536], BF16)
            wglu = AR.alloc_tail([4, 512], BF16)
            wbs = AR.alloc([4, D], BF16)
            wout = AR.alloc([8, D], BF16)
            modS = [AR.alloc([D], F32) for _ in range(3)]
            xs2 = [AR.alloc([D], F32) for _ in range(3)]
            battn = AR.alloc([D], F32)
            ytm = AR.alloc([512], F32)
            e1_2 = [AR.alloc([512], F32) for _ in range(2)]
            hb2 = [AR.alloc([D], BF16) for _ in range(2)]
            hT2 = [AR.alloc([8, 128], BF16) for _ in range(2)]
            UT2 = [AR.alloc([512], BF16) for _ in range(2)]
            Dsb2 = [AR.alloc([2, 16, 32], F32) for _ in range(2)]
            HQ = AR.alloc([2, 2, 16, 32], F32)
            Hs = HQ[:, 0]
            Qs = HQ[:, 1]
            car = AR.alloc([2, 16], F32)
            Sprev = AR.alloc([2, 16, 32], BF16)
            tt = [AR.alloc([2, 16], F32) for _ in range(2)]
            e2 = AR.alloc([512], F32)
            sgl = AR.alloc([4, 128], BF16)
            ssmg = AR.alloc([4, 128], BF16)
            sgs2 = [AR.alloc([8, 128], BF16) for _ in range(2)]
            mrg = AR.alloc([8, 128], BF16)
            Sfin = AR.alloc([2, 128], F32)
            mrg32 = e2.rearrange("p (a b) -> p a b", a=4)

            cast_load_rows(wbs, w_bs, 4, 'wbs')
            cast_load_rows(wout, w_out, 8, 'wout')
            op('dve', MS(car, 0.0), [], ['car'])
            cos2 = cosT.rearrange("p g n -> p (g n)").unsqueeze(1).to_broadcast([128, 2, 512])
            sin1 = sinT.rearrange("p g n -> p (g n)")
            rp1 = rpow.rearrange("p g n -> p (g n)")
            Lre2 = L4[:, 0, :]
            Lim_ = L4[:, 1, :]
            nLim = L4[:, 2, :]

            def load_x(T_):
                s3 = T_ % 3
                dma_load(xs2[s3], xin[T_], ('xs', s3), 'xS%d' % s3)

            def stApre(T_):
                ty = type_of(T_)
                sl = T_ % 2
                if T_ == ntp:
                    load_mod(modS, 1, [0, 1])
                xkey = ('xs', T_ % 3)
                if T_ + 1 < NT:
                    load_x(T_ + 1)
                hb = hb2[sl]
                hT = hT2[sl]
                Dsb = Dsb2[sl]
                frontend(xs2[T_ % 3], xkey, modS[1], modS[0], [('modA', 1)], [('modA', 0)], hb, hT, None, hbk=('hb', sl), hTk=('hT', sl), add_eng='pool')
                bu = nb()
                pe_group([MM(bank(bu), hT[:, k, :], winS[:, k, 0:512], k == 0, k == 7) for k in range(8)], [('hT', sl), 'winS0'], [pk(bu)])
                op('dve', TT(e1_2[sl], bank(bu), dbc, MUL), [pk(bu), 'dbc'], [('e1', sl)])
                op('dve', lambda e, bu=bu: e.transpose(out=ytm, in_=bank(bu)), [pk(bu)], ['ytm'])
                op('act', ACP(UT2[sl], ytm), ['ytm'], [('UT', sl)])
                for hh in range(2):
                    bb = nb()
                    fns = []
                    for mi in range(4):
                        m = hh * 4 + mi
                        for k in range(8):
                            fns.append(MM(bank(bb)[:, mi * 128:(mi + 1) * 128], winS[:, k, 512 + m * 128:512 + (m + 1) * 128], hT[:, k, :], k == 0, k == 7))
                    pe_group(fns, [('hT', sl), 'winS1'], [pk(bb)])
                    op('act', ACT(sgs2[sl][:, hh * 4:(hh + 1) * 4, :].rearrange("p a b -> p (a b)"), bank(bb), AF.Sigmoid), [pk(bb)], [('sgs', sl, hh)])
                for part, Wst, wkey in ((0, WstRe, 'WstRe'), (1, WstIm, 'WstIm')):
                    bd = nb()
                    fns = []
                    for gl in range(16):
                        for gh in range(2):
                            g = gh * 16 + gl
                            fns.append(MM(bank(bd)[:, gl * 32:(gl + 1) * 32], Wst[:, g, :], UT2[sl][:, (g // 2) * 32:(g // 2 + 1) * 32], gh == 0, gh == 1))
                    pe_group(fns, [('UT', sl), wkey], [pk(bd)])
                    op('act', ACP(Dsb[:, part].rearrange("p g n -> p (g n)"), bank(bd)), [pk(bd)], [('Dsb', sl, part)])

            def stRec(T_):
                ty = type_of(T_)
                sl = T_ % 2
                dkeys = [('Dsb', sl, 0), ('Dsb', sl, 1)]
                Dsb = Dsb2[sl]
                t0 = tt[0]
                t1 = tt[1]
                if ty == 0:
                    D2 = Dsb.rearrange("p c g n -> p c (g n)")
                    H2 = Hs.rearrange("p c g n -> p c (g n)")
                    Q2 = Qs.rearrange("p c g n -> p c (g n)")
                    op('dve', TT(t0, car, Lre2.unsqueeze(1).to_broadcast([128, 2, 16]), MUL), ['car', 'L4'], ['t0'])
                    op('dve', TT(t1[:, 0, :], car[:, 1, :], nLim, MUL), ['car', 'L4'], ['t1'])
                    op('dve', TT(t1[:, 1, :], car[:, 0, :], Lim_, MUL), ['car', 'L4'], ['t1'])
                    op('dve', TT(t0, t0, t1, ADD), ['t1'], ['t0'])
                    op('dve', TT(Dsb[:, :, :, 0], Dsb[:, :, :, 0], t0, ADD), ['t0'], dkeys)
                    op('dve', TT(H2, D2, cos2, MUL), dkeys + ['tabs'], ['Hs'])
                    op('pool', TT(Q2[:, 0, :], D2[:, 1, :], sin1, MUL), dkeys + ['tabs'], [('Qs', 0)])
                    op('pool', TT(Q2[:, 1, :], D2[:, 0, :], sin1, MUL), dkeys + ['tabs'], [('Qs', 1)])
                    op('dve', TT(H2[:, 0, :], H2[:, 0, :], Q2[:, 0, :], ADD), [('Qs', 0)], ['Hs'])
                    op('dve', TT(H2[:, 1, :], H2[:, 1, :], Q2[:, 1, :], SUB), [('Qs', 1)], ['Hs'])
                    for c_ in range(2):
                        op('dve', withcost((lambda c_: lambda e: e.tensor_tensor_scan(out=Q2[:, c_, :], data0=rp1, data1=H2[:, c_, :], initial=0.0,
                                                                                      op0=MUL, op1=ADD))(c_), 1.3), ['Hs', 'tabs'], [('Qs', c_)])
                    op('dve', TT(D2, Q2, cos2, MUL), [('Qs', 0), ('Qs', 1), 'tabs'], dkeys)
                    op('pool', TT(H2[:, 0, :], Q2[:, 1, :], sin1, MUL), [('Qs', 1), 'tabs'], ['Hs'])
                    op('pool', TT(H2[:, 1, :], Q2[:, 0, :], sin1, MUL), [('Qs', 0), 'tabs'], ['Hs'])
                    op('dve', TT(D2[:, 0, :], D2[:, 0, :], H2[:, 0, :], SUB), ['Hs'], dkeys)
                    op('dve', TT(D2[:, 1, :], D2[:, 1, :], H2[:, 1, :], ADD), ['Hs'], dkeys)
                    op('act', ACP(Sprev[:, :, :, 1:32], Dsb[:, :, :, 0:31]), dkeys, ['Sprev'])
                    op('act', ACP(Sprev[:, :, :, 0], car), ['car'], ['Sprev'])
                    op('dve', CP(car, Dsb[:, :, :, 31]), dkeys, ['car'])
                    if T_ == ntp - 1:
                        bf_ = nb()
                        pe_group([TR(bank(bf_)[0:16, part * 128:(part + 1) * 128], car[:, part, :], ident) for part in range(2)],
                                 ['car', 'ident'], [pk(bf_)])
                        op('act', ACP(Sfin[0:16].rearrange("p a b -> p (a b)"), bank(bf_)[0:16, 0:256]), [pk(bf_)], ['Sfin'])
                        for part, dst in ((0, o_srep), (1, o_simp)):
                            dma_store(dst.rearrange("(gh gl) p -> gl gh p", gh=2), Sfin[0:16, part, :].rearrange("g (gh p) -> g gh p", gh=2), ['Sfin'])
                else:
                    assert sl == 0
                    dfl = Dsb2[1].rearrange("p c g n -> p (c g n)")
                    SiT = dfl[:, 0:512].rearrange("p (c g b) -> p c g b", c=2, g=16)
                    S1T = dfl[:, 512:1024].rearrange("p (c g b) -> p c g b", c=2, g=16)
                    stg = HQ.rearrange("p a c g n -> p (a c g n)")
                    stg3 = stg[0:16, :].rearrange("b (gl gh p) -> b gl gh p", gl=16, gh=2)
                    qk = ['Hs', ('Qs', 0), ('Qs', 1)]
                    for part, src in ((0, ssre), (1, ssim)):
                        def ld_st(e, s_, src=src):
                            sv = src.rearrange("b (gh gl p) -> b gh gl p", gh=2, gl=16)
                            for gh in range(2):
                                e.dma_start(out=stg3[:, :, gh, :], in_=sv[:, gh]).then_inc(s_, 16)
                        S.add('sp', ld_st, reads=[], writes=qk, dma=uk(), ndma=2)
                        b = nb()
                        pe_group([TR(bank(b)[:, gl * 16:(gl + 1) * 16], stg[0:16, gl * 128:(gl + 1) * 128], ident[0:16, 0:16]) for gl in range(16)],
                                 qk + ['ident'], [pk(b)])
                        op('act', ACP(SiT[:, part], bank(b)[:, 0:256].rearrange("p (g b) -> p g b", g=16)), [pk(b)], [('Dsb', 1, 0)])
                    Dv = Dsb.rearrange("p c g (b h) -> p c g b h", h=2)
                    lre_b = Lre2.unsqueeze(2).to_broadcast([128, 16, 16])
                    lim_b = Lim_.unsqueeze(2).to_broadcast([128, 16, 16])
                    w0 = e2[:, 0:256].rearrange("p (g b) -> p g b", g=16)
                    w1 = ytm[:, 0:256].rearrange("p (g b) -> p g b", g=16)

                    def cstep(Sa, Sb, hsel, ka, kb):
                        op('dve', TT(w0, Sa[:, 0], lre_b, MUL), [ka, 'L4'], ['e2'])
                        op('dve', TT(w1, Sa[:, 1], lim_b, MUL), [ka, 'L4'], ['ytm'])
                        op('dve', TT(w0, w0, w1, SUB), ['ytm'], ['e2'])
                        op('dve', TT(Sb[:, 0], w0, Dv[:, 0, :, :, hsel], ADD), ['e2'] + dkeys, [kb])
                        op('dve', TT(w0, Sa[:, 1], lre_b, MUL), [ka, 'L4'], ['e2'])
                        op('dve', TT(w1, Sa[:, 0], lim_b, MUL), [ka, 'L4'], ['ytm'])
                        op('dve', TT(w0, w0, w1, ADD), ['ytm'], ['e2'])
                        op('dve', TT(Sb[:, 1], w0, Dv[:, 1, :, :, hsel], ADD), ['e2'] + dkeys, [kb])
                    Spv = Sprev.rearrange("p c g (b h) -> p c g b h", h=2)
                    op('dve', CP(Spv[:, :, :, :, 0], SiT), [('Dsb', 1, 0)], ['Sprev'])
                    cstep(SiT, S1T, 0, ('Dsb', 1, 0), ('Dsb', 1, 1))
                    op('dve', CP(Spv[:, :, :, :, 1], S1T), [('Dsb', 1, 1)], ['Sprev'])
                    cstep(S1T, SiT, 1, ('Dsb', 1, 1), ('Dsb', 1, 0))
                    for part, dst in ((0, o_sres), (1, o_sims)):
                        for gq in range(4):
                            bo_ = nb()
                            pe_group([TR(bank(bo_)[0:16, gi * 128:(gi + 1) * 128], SiT[:, part, gq * 4 + gi, :], ident) for gi in range(4)],
                                     [('Dsb', 1, 0), 'ident'], [pk(bo_)])
                            op('act', ACP(stg[0:16, gq * 512:(gq + 1) * 512], bank(bo_)[0:16, :]), [pk(bo_)], qk)
                        def st_st(e, s_, dst=dst):
                            dv_ = dst.rearrange("b (gh gl p) -> b gh gl p", gh=2, gl=16)
                            for gh in range(2):
                                e.dma_start(out=dv_[:, gh], in_=stg3[:, :, gh, :]).then_inc(s_, 16)
                        S.add('sp', st_st, reads=qk, writes=[], dma=uk(), ndma=2)

            def stB1(T_):
                sl = T_ % 2
                by = 0
                fns = []
                for gp in range(16):
                    o_ = bank(by)[:, gp * 32:(gp + 1) * 32]
                    fns.append(MM(o_, Toep[:, gp, :], UT2[sl][:, gp * 32:(gp + 1) * 32], True, False))
                    for x in range(2):
                        g = 2 * gp + x
                        fns.append(MM(o_, CLre[:, g, :], Sprev[:, 0, g % 16, :], False, False))
                        fns.append(MM(o_, CLim[:, g, :], Sprev[:, 1, g % 16, :], False, x == 1))
                pe_group(fns, [('UT', sl), 'Toep', 'CL', 'Sprev'], [pk(by)])

            def stB2(T_):
                ty = type_of(T_)
                sl = T_ % 2
                xkey = ('xs', T_ % 3)
                e1 = e1_2[sl]
                ssmb = hb2[sl][:, 0:512]
                ssmT = hb2[sl][:, 512:1024].rearrange("p (a b) -> p a b", a=4)
                hbk_ = ('hb', sl)
                if T_ == ntp:
                    load_mod(modS[2:3], 1, [2])
                op('dve', lambda e: e.transpose(out=ytm, in_=bank(0)), [pk(0)], ['ytm'])
                op('pool', TT(ytm, ytm, e1, ADD), [('e1', sl)], ['ytm'])
                op('act', ACT(e1, ytm, AF.Square), ['ytm'], [('e1', sl)])
                op('act', withcost(lambda e, e1=e1: e.activation(out=e1, in_=e1, func=AF.Identity, scale=0.044715, bias=1.0), ('tt', 512)), [], [('e1', sl)])
                op('dve', TT(e1, e1, ytm, MUL), ['ytm'], [('e1', sl)])
                op('act', ACT(e2, e1, AF.Sigmoid, scale=1.5957691216057308), [('e1', sl)], ['e2'])
                op('dve', TT(ssmb, ytm, e2, MUL), ['ytm', 'e2'], [hbk_])
                bst = nb()
                pe_group([TR(bankbf(bst)[:, c * 128:(c + 1) * 128], ssmb[:, c * 128:(c + 1) * 128], identb) for c in range(4)], [hbk_, 'identb'], [pk(bst)])
                op('act', ACP(ssmT.rearrange("p a b -> p (a b)"), bankbf(bst)[:, 0:512]), [pk(bst)], [hbk_])
                bgl = nb()
                fns = []
                for m in range(4):
                    for kc in range(4):
                        fns.append(MM(bank(bgl)[:, m * 128:(m + 1) * 128], wglu[:, kc, m * 128:(m + 1) * 128], ssmT[:, kc, :], kc == 0, kc == 3))
                pe_group(fns, [hbk_, 'wglu'], [pk(bgl)])
                op('act', ACT(sgl.rearrange("p a b -> p (a b)"), bank(bgl), AF.Sigmoid), [pk(bgl)], ['sgl'])
                op('dve', TT(ssmg, ssmT, sgl, MUL), [hbk_, 'sgl'], ['ssmg'])
                for hh in range(2):
                    bb = nb()
                    fns = []
                    for mi in range(4):
                        m = hh * 4 + mi
                        for kc in range(4):
                            fns.append(MM(bank(bb)[:, mi * 128:(mi + 1) * 128], wbs[:, kc, m * 128:(m + 1) * 128], ssmg[:, kc, :], kc == 0, kc == 3))
                    pe_group(fns, ['ssmg', 'wbs'], [pk(bb)])
                    op('dve', TT(mrg32, bank(bb).rearrange("p (a b) -> p a b", a=4), sgs2[sl][:, hh * 4:(hh + 1) * 4, :], MUL), [pk(bb), ('sgs', sl, hh)], ['e2'])
                    op('dve', TT(mrg[:, hh * 4:(hh + 1) * 4, :], mrg32, crT[:, hh * 4:(hh + 1) * 4, T_ * 128:(T_ + 1) * 128], ADD), ['e2'], [('mrg', hh)])
                ba = 6
                for hf in range(2):
                    pe_group([MM(bank(ba + hf), mrg[:, k, :], wout[:, k, hf * 512:(hf + 1) * 512], k == 0, k == 7) for k in range(8)],
                             [('mrg', 0), ('mrg', 1), 'wout'], [pk(ba + hf)])
                pa = PS[:, 512 * ba:512 * (ba + 2)]
                op('dve', TT(battn, pa, modS[2], MUL), [pk(ba), pk(ba + 1), ('modA', 2)], ['battn'])
                op('pool', TT(battn, battn, xs2[T_ % 3], ADD), [xkey], ['battn'])
                dma_store(x1d[T_], battn, ['battn'], 'x1d')

            pslo[0] = 1
            pshi[0] = 6
            load_x(0)
            load_mod(modS, 0, [0, 1, 2])
            stApre(0)
            stRec(0)
            for T_ in range(NT):
                if T_ + 1 < NT:
                    stApre(T_ + 1)
                stB1(T_)
                if T_ + 1 < NT:
                    stRec(T_ + 1)
                stB2(T_)
            pslo[0] = 0
            pshi[0] = 8

        def phase2():
            AR.reset()
            wup = AR.alloc([8, 4 * D], BF16)
            wdn = AR.alloc([32, D], BF16)
            modB = [AR.alloc([D], F32) for _ in range(3)]
            gfb = AR.alloc([D], F32)
            xs3 = [AR.alloc([D], F32) for _ in range(3)]
            junkF = AR.alloc([D], F32)
            hbF = AR.alloc([D], BF16)
            hT2 = [AR.alloc([8, 128], BF16) for _ in range(3)]
            rl2 = [AR.alloc([512], BF16) for _ in range(2)]
            hidT2 = [AR.alloc([32, 128], BF16) for _ in range(3)]
            junkB = AR.alloc([D], F32)
            hbB = AR.alloc([D], BF16)
            yo = AR.alloc([D], F32)
            for c4 in range(4):
                cast_load_rows(wup[:, :, c4 * D:(c4 + 1) * D], w_up, 8, ('wup', c4), c4 * D, (c4 + 1) * D)
            for c4 in range(4):
                cast_load_rows(wdn[:, c4 * 8:(c4 + 1) * 8, :], w_down[c4 * 1024:(c4 + 1) * 1024, :], 8, ('wdn', c4))
            for w3 in range(3):
                dma_load(modB[w3], modd[0, 3 + w3], ('modB', w3))
            dma_load(gfb, nfg.partition_broadcast(128), 'gfb')

            def load_x1(T_):
                s3 = T_ % 3
                dma_load(xs3[s3], x1d[T_], ('xs', s3), 'x2l%d' % s3)

            def stageA(T_, part):
                s3 = T_ % 3
                if part == 1:
                    if T_ + 1 < NT:
                        load_x1(T_ + 1)
                    if T_ == ntp:
                        for w3 in range(2):
                            dma_load(modB[w3], modd[1, 3 + w3], ('modB', w3))
                    frontend(xs3[s3], ('xs', s3), modB[1], modB[0], [('modB', 1)], [('modB', 0)], hbF, hT2[T_ % 3], junkF,
                             jk=('junkF',), hbk='hbF', hTk=('hT', T_ % 3), part=1)
                else:
                    frontend2(hbF, hT2[T_ % 3], 'hbF', ('hT', T_ % 3))

            def stageB(T_):
                sl = T_ % 3
                s3 = T_ % 3
                hT = hT2[sl]
                hidT = hidT2[sl]
                if T_ == ntp:
                    dma_load(modB[2], modd[1, 5], ('modB', 2))
                if T_ + 1 < NT:
                    stageA(T_ + 1, 1)
                for f4 in range(8):
                    rl = rl2[f4 % 2]
                    bb = nb()
                    fns = []
                    for fi in range(4):
                        f = f4 * 4 + fi
                        for k in range(8):
                            fns.append(MM(bank(bb)[:, fi * 128:(fi + 1) * 128], wup[:, k, f * 128:(f + 1) * 128], hT[:, k, :], k == 0, k == 7))
                    pe_group(fns, [('hT', sl), ('wup', f4 // 2)], [pk(bb)])
                    op('act', ACT(rl, bank(bb), AF.Relu), [pk(bb)], [('rl', f4 % 2)])
                    op('dve', TT(hidT[:, f4 * 4:(f4 + 1) * 4, :].rearrange("p a b -> p (a b)"), rl, rl, MUL), [('rl', f4 % 2)], [('hidT', sl, f4)])
                if T_ + 1 < NT:
                    stageA(T_ + 1, 2)
                bd2 = nb2()
                for hf in range(2):
                    pe_group([MM(bank(bd2 + hf), hidT[:, f, :], wdn[:, f, hf * 512:(hf + 1) * 512], f == 0, f == 31) for f in range(32)],
                             [('hidT', sl, f4) for f4 in range(8)] + [('wdn', c4) for c4 in range(4)], [pk(bd2 + hf)])
                pd = PS[:, 512 * bd2:512 * (bd2 + 2)]
                op('dve', TT(junkB, pd, modB[2], MUL), [pk(bd2), pk(bd2 + 1), ('modB', 2)], ['junkB'])
                op('dve', TT(junkB, junkB, xs3[s3], ADD), [('xs', s3)], ['junkB'])
                op('act', ACT(hbB, junkB, AF.Square, accum_out=stat[:, 24:25]), ['junkB'], ['hbB', 'fs0'])
                op('dve', TS(stat[:, 25:26], stat[:, 24:25], 1.0 / D, EPS, MUL, ADD), ['fs0'], ['fs1'])
                op('pool', TT(stat[:, 26:27], stat[:, 25:26], nhalf[:, 0:1], ALU.pow), ['fs1', 'nhalf'], ['fs2'])
                op('dve', STT(yo, junkB, stat[:, 26:27], gfb, MUL, MUL), ['junkB', 'fs2', 'gfb'], ['yo'])
                dma_store(yout[T_], yo, ['yo'], 'yout')

            load_x1(0)
            stageA(0, 1)
            stageA(0, 2)
            for T_ in range(NT):
                stageB(T_)

        import os as _os
        _stop = int(_os.environ.get('KSTOP', '9'))
        _maxops = int(_os.environ.get('KMAXOPS', '0'))
        s5gen = s5setup()
        next(s5gen)
        phase0()
        for _ in s5gen:
            pass
        if _stop >= 1:
            S.barrier()
            passR()
        if _stop >= 3:
            S.barrier()
            passS()
        if _stop >= 4:
            S.barrier()
            phase2()
        print("n_ops", len(S.ops), "n_dma_keys", len(S.dma_keys))
        S.emit(nc)
    return nc


_CACHE = {}


def make_in_maps(inputs, ntp):
    consts, cd = host_consts(ntp)
    p2t = perm_r2t()
    f = lambda a: np.ascontiguousarray(np.asarray(a, dtype=np.float32))
    xp = f(inputs['x_prompt']); xsm = f(inputs['x_sample'])
    cp = f(inputs['c_prompt']); csm = f(inputs['c_sample'])
    maps = []
    shared = {
        'w_ada': f(inputs['w_ada'][0]), 'b_ada': f(inputs['b_ada'][0]).reshape(1, -1),
        'norm1_g': f(inputs['norm1_g'][0]).reshape(1, -1), 'norm2_g': f(inputs['norm2_g'][0]).reshape(1, -1),
        'norm_f_g': f(inputs['norm_f_g']).reshape(1, -1), 'w_in': f(inputs['w_in'][0]),
        'lam_re': f(inputs['ssm_lambda_re'][0]), 'lam_im': f(inputs['ssm_lambda_im'][0]),
        'log_dt': f(inputs['ssm_log_dt'][0]).reshape(1, 32),
        'b_re': f(inputs['ssm_b_re'][0]), 'b_im': f(inputs['ssm_b_im'][0]),
        'c_re': f(inputs['ssm_c_re'][0]).reshape(512, 64), 'c_im': f(inputs['ssm_c_im'][0]).reshape(512, 64),
        'ssm_d': f(inputs['ssm_d'][0]).reshape(1, 512),
        'w_glu': f(inputs['w_glu'][0]), 'w_br': f(inputs['w_br'][0]), 'w_bs': f(inputs['w_bs'][0]),
        'w_out': f(inputs['w_out'][0]), 'w_up': f(inputs['w_up'][0]), 'w_down': f(inputs['w_down'][0]),
        'k_ident': consts['ident'], 'k_rot': consts['rot'],
        'k_maskT': consts['maskT'].reshape(2, 128, 512), 'k_qin': consts['qin'].reshape(2, 128, 512),
        'k_kout': consts['kout'], 'k_seqf': consts['seqf'].reshape(128, 2048), 'k_seqp': consts['seqp'],
        'k_tmask': consts['tmask'],
    }
    for c in range(NCORE):
        xt = np.concatenate([xp[c, :ntp * 128].reshape(ntp, 128, D), xsm[16 * c:16 * c + 16].reshape(1, 128, D)], axis=0)
        xt = np.ascontiguousarray(xt[:, p2t, :])
        ce = np.stack([np.broadcast_to(cp[c], (128, D)), np.repeat(csm[16 * c:16 * c + 16], 8, axis=0)[p2t]])
        m = dict(shared)
        m['xin'] = xt
        m['cexp'] = np.ascontiguousarray(ce)
        m['sret'] = f(inputs['state_ret'][0, 16 * c:16 * c + 16])
        m['ssre'] = f(inputs['state_ssm_re'][0, 16 * c:16 * c + 16]).reshape(16, 2048)
        m['ssim'] = f(inputs['state_ssm_im'][0, 16 * c:16 * c + 16]).reshape(16, 2048)
        maps.append(m)
    return maps, cd


def assemble(results, ntp):
    p2t = perm_r2t()
    inv = np.argsort(p2t)
    B = NCORE
    yp = np.zeros((B, ntp * 128, D), np.float32)
    ys = np.zeros((128, 8, D), np.float32)
    retp = np.zeros((1, B, 4, 128, 128), np.float32)
    srep = np.zeros((1, B, 32, 64), np.float32)
    simp = np.zeros((1, B, 32, 64), np.float32)
    rets = np.zeros((1, 128, 4, 128, 128), np.float32)
    sres = np.zeros((1, 128, 32, 64), np.float32)
    sims = np.zeros((1, 128, 32, 64), np.float32)
    for c in range(B):
        r = results[c]
        y = np.asarray(r['yout'])[:, inv, :]
        yp[c] = y[:ntp].reshape(ntp * 128, D)
        ys[16 * c:16 * c + 16] = y[ntp].reshape(16, 8, D)
        retp[0, c] = r['o_retp']
        srep[0, c] = r['o_srep']
        simp[0, c] = r['o_simp']
        rets[0, 16 * c:16 * c + 16] = r['o_rets']
        sres[0, 16 * c:16 * c + 16] = np.asarray(r['o_sres']).reshape(16, 32, 64)
        sims[0, 16 * c:16 * c + 16] = np.asarray(r['o_sims']).reshape(16, 32, 64)
    return (yp, ys, retp, srep, simp, rets, sres, sims)


def kernel(**inputs):
    ntp = 16
    maps, cd = make_in_maps(inputs, ntp)
    if ntp not in _CACHE:
        _CACHE[ntp] = build_program(ntp, cd)
    nc = _CACHE[ntp]
    res = run_bass_kernel_spmd(nc, maps, core_ids=list(range(NCORE)))
    return assemble(res.results, ntp)
```
